# Optimizing a Trainium2 kernel written in Bass

```python
import math
import jax, jax.numpy as jnp
from jax import lax
import numpy as np

D_MODEL = 1024
BATCH = 8
SEQ = 2048
DEPTH = 2

N_A_LAYERS = DEPTH // 2
N_B_LAYERS = DEPTH - N_A_LAYERS
POOL_WINDOWS = (2, 4, 8, 16)
N_POOL_GROUPS = len(POOL_WINDOWS)
POOL_GROUP_DIM = D_MODEL // N_POOL_GROUPS
HEAD_DIM = 64
N_HEADS = D_MODEL // HEAD_DIM
MOBA_BLOCK = 256
MOBA_TOP_K = 3
QUERY_CHUNK = 16
D_FF = 4 * D_MODEL
ROPE_THETA = 10000.0
LN_EPS = 1e-5
DEEPNORM_ALPHA = (2.0 * DEPTH) ** 0.25
DEEPNORM_BETA = (8.0 * DEPTH) ** -0.25

kernel_name = "yoco_pool_moba_hybrid"


def layer_norm(x, g, b):
    xf = x.astype(jnp.float32)
    mu = jnp.mean(xf, axis=-1, keepdims=True)
    var = jnp.mean(jnp.square(xf - mu), axis=-1, keepdims=True)
    return ((xf - mu) * lax.rsqrt(var + LN_EPS) * g + b).astype(x.dtype)


def deepnorm_residual(x, y, g, b):
    return layer_norm(DEEPNORM_ALPHA * x + y, g, b)


def sq_relu_mlp(x, w1, w2):
    h = jnp.square(jax.nn.relu(x @ w1))
    return h @ w2


def rope(x):
    s, dh = x.shape[1], x.shape[-1]
    half = dh // 2
    inv_freq = ROPE_THETA ** (-jnp.arange(half, dtype=jnp.float32) * 2.0 / dh)
    ang = jnp.arange(s, dtype=jnp.float32)[:, None] * inv_freq[None, :]
    cos = jnp.cos(ang)[None, :, None, :].astype(x.dtype)
    sin = jnp.sin(ang)[None, :, None, :].astype(x.dtype)
    x1, x2 = x[..., :half], x[..., half:]
    return jnp.concatenate([x1 * cos - x2 * sin, x2 * cos + x1 * sin], axis=-1)


def pool_mixer(x, w_pool, scale):
    b, s, d = x.shape
    cs = jnp.cumsum(x.astype(jnp.float32), axis=1)
    cs0 = jnp.pad(cs, ((0, 0), (1, 0), (0, 0)))
    t = jnp.arange(s)
    groups = []
    for g, w in enumerate(POOL_WINDOWS):
        csg = cs0[:, :, g * POOL_GROUP_DIM:(g + 1) * POOL_GROUP_DIM]
        upper = csg[:, 1:]
        lower = jnp.pad(csg, ((0, 0), (w, 0), (0, 0)))[:, 1:s + 1]
        count = jnp.minimum(t + 1, w).astype(jnp.float32)[None, :, None]
        xg = x[:, :, g * POOL_GROUP_DIM:(g + 1) * POOL_GROUP_DIM].astype(jnp.float32)
        groups.append((upper - lower) / count - xg)
    pooled = jnp.stack(groups, axis=2).astype(x.dtype)
    y = jnp.einsum('bsgc,gce->bsge', pooled, w_pool).reshape(b, s, d)
    return y * scale


def shared_kv(x, w_kv):
    b, s, d = x.shape
    kv = x @ w_kv
    k = rope(kv[..., :d].reshape(b, s, N_HEADS, HEAD_DIM))
    v = kv[..., d:].reshape(b, s, N_HEADS, HEAD_DIM)
    n_blk = -(-s // MOBA_BLOCK)
    pad = n_blk * MOBA_BLOCK - s
    k = jnp.pad(k, ((0, 0), (0, pad), (0, 0), (0, 0)))
    v = jnp.pad(v, ((0, 0), (0, pad), (0, 0), (0, 0)))
    k_blocks = k.reshape(b, n_blk, MOBA_BLOCK, N_HEADS, HEAD_DIM).transpose(0, 3, 1, 2, 4)
    v_blocks = v.reshape(b, n_blk, MOBA_BLOCK, N_HEADS, HEAD_DIM).transpose(0, 3, 1, 2, 4)
    k_mean = jnp.mean(k_blocks.astype(jnp.float32), axis=3).astype(x.dtype)
    return k_blocks, v_blocks, k_mean


def moba_attention(x, w_q, w_o, k_blocks, v_blocks, k_mean):
    b, s, d = x.shape
    q = rope((x @ w_q).reshape(b, s, N_HEADS, HEAD_DIM)).transpose(0, 2, 1, 3)
    n_blk = k_blocks.shape[2]
    n_sel = min(MOBA_TOP_K, n_blk - 1)
    scale = HEAD_DIM ** -0.5
    bi = jnp.arange(b)[:, None, None, None]
    hi = jnp.arange(N_HEADS)[None, :, None, None]
    span = n_sel * MOBA_BLOCK

    def chunk(c):
        p0 = c * QUERY_CHUNK
        q_c = lax.dynamic_slice_in_dim(q, p0, QUERY_CHUNK, axis=2)
        q_pos = p0 + jnp.arange(QUERY_CHUNK)
        own = p0 // MOBA_BLOCK
        k_own = lax.dynamic_index_in_dim(k_blocks, own, axis=2, keepdims=False)
        v_own = lax.dynamic_index_in_dim(v_blocks, own, axis=2, keepdims=False)
        k_pos = own * MOBA_BLOCK + jnp.arange(MOBA_BLOCK)
        s_own = jnp.einsum('bhqd,bhkd->bhqk', q_c, k_own).astype(jnp.float32) * scale
        s_own = jnp.where(k_pos[None, :] <= q_pos[:, None], s_own, -jnp.inf)
        if n_sel > 0:
            gate = jnp.einsum('bhqd,bhnd->bhqn', q_c, k_mean).astype(jnp.float32)
            gate = jnp.where(jnp.arange(n_blk) < own, gate, -jnp.inf)
            _, sel = lax.top_k(gate, n_sel)
            sel_ok = sel < own
            k_sel = k_blocks[bi, hi, sel]
            v_sel = v_blocks[bi, hi, sel]
            s_sel = jnp.einsum('bhqd,bhqnkd->bhqnk', q_c, k_sel).astype(jnp.float32) * scale
            s_sel = jnp.where(sel_ok[..., None], s_sel, -jnp.inf).reshape(b, N_HEADS, QUERY_CHUNK, span)
            p = jax.nn.softmax(jnp.concatenate([s_sel, s_own], axis=-1), axis=-1).astype(v_blocks.dtype)
            p_sel = p[..., :span].reshape(b, N_HEADS, QUERY_CHUNK, n_sel, MOBA_BLOCK)
            o = (jnp.einsum('bhqnk,bhqnkd->bhqd', p_sel, v_sel)
                 + jnp.einsum('bhqk,bhkd->bhqd', p[..., span:], v_own))
        else:
            p = jax.nn.softmax(s_own, axis=-1).astype(v_blocks.dtype)
            o = jnp.einsum('bhqk,bhkd->bhqd', p, v_own)
        return o

    out = lax.map(chunk, jnp.arange(s // QUERY_CHUNK))
    out = out.transpose(1, 0, 3, 2, 4).reshape(b, s, d)
    return out @ w_o


def setup_inputs(seed: int = 0) -> dict:
    key = jax.random.key(seed)
    ks = jax.random.split(key, 12)
    f32 = jnp.float32
    d = D_MODEL
    x = jax.random.normal(ks[0], (BATCH, SEQ, d), f32)
    pool_w = jax.random.normal(ks[1], (N_A_LAYERS, N_POOL_GROUPS, POOL_GROUP_DIM, POOL_GROUP_DIM), f32) * (POOL_GROUP_DIM ** -0.5) * DEEPNORM_BETA
    pool_scale = 1.0 + 0.1 * jax.random.normal(ks[2], (N_A_LAYERS, d), f32)
    mlp_w1 = jax.random.normal(ks[3], (DEPTH, d, D_FF), f32) * (d ** -0.5) * DEEPNORM_BETA
    mlp_w2 = jax.random.normal(ks[4], (DEPTH, D_FF, d), f32) * (D_FF ** -0.5) * DEEPNORM_BETA
    ln_mix_g = 1.0 + 0.05 * jax.random.normal(ks[5], (DEPTH, d), f32)
    ln_mix_b = 0.02 * jax.random.normal(ks[6], (DEPTH, d), f32)
    ln_ffn_g = 1.0 + 0.05 * jax.random.normal(ks[7], (DEPTH, d), f32)
    ln_ffn_b = 0.02 * jax.random.normal(ks[8], (DEPTH, d), f32)
    kv_scale = jnp.concatenate([jnp.ones((d,), f32), jnp.full((d,), DEEPNORM_BETA, f32)])
    w_kv = jax.random.normal(ks[9], (d, 2 * d), f32) * (d ** -0.5) * kv_scale[None, :]
    w_q = jax.random.normal(ks[10], (N_B_LAYERS, d, d), f32) * (d ** -0.5)
    w_o = jax.random.normal(ks[11], (N_B_LAYERS, d, d), f32) * (d ** -0.5) * DEEPNORM_BETA
    return {"x": x, "pool_w": pool_w, "pool_scale": pool_scale, "mlp_w1": mlp_w1, "mlp_w2": mlp_w2,
            "ln_mix_g": ln_mix_g, "ln_mix_b": ln_mix_b, "ln_ffn_g": ln_ffn_g, "ln_ffn_b": ln_ffn_b,
            "w_kv": w_kv, "w_q": w_q, "w_o": w_o}


def reference(x, pool_w, pool_scale, mlp_w1, mlp_w2, ln_mix_g, ln_mix_b, ln_ffn_g, ln_ffn_b, w_kv, w_q, w_o):
    kv_cache = None
    for l in range(DEPTH):
        if l < N_A_LAYERS:
            y = pool_mixer(x, pool_w[l], pool_scale[l])
        else:
            if kv_cache is None:
                kv_cache = shared_kv(x, w_kv)
            k_blocks, v_blocks, k_mean = kv_cache
            j = l - N_A_LAYERS
            y = moba_attention(x, w_q[j], w_o[j], k_blocks, v_blocks, k_mean)
        x = deepnorm_residual(x, y, ln_mix_g[l], ln_mix_b[l])
        x = deepnorm_residual(x, sq_relu_mlp(x, mlp_w1[l], mlp_w2[l]), ln_ffn_g[l], ln_ffn_b[l])
    return x
```

```python
import numpy as np
from contextlib import ExitStack
import concourse.bass as bass
import concourse.mybir as mybir
from concourse.bass_utils import run_bass_kernel_spmd

F32 = mybir.dt.float32
BF16 = mybir.dt.bfloat16
ALU = mybir.AluOpType
AF = mybir.ActivationFunctionType
AX = mybir.AxisListType

S = 2048
D = 1024
DFF = 4096
NT = S // 128
NC_ = D // 128
H = 16
DH = 64
BLK = 256
NBLK = S // BLK
ALPHA = 4.0 ** 0.25
EPS = 1e-5
NEG = -30000.0
N_CORES = 8


class _Op:
    __slots__ = ("eng", "fn", "deps", "inc", "val", "dma")

    def __init__(self, eng, fn, dma):
        self.eng = eng
        self.fn = fn
        self.deps = []
        self.inc = False
        self.val = None
        self.dma = dma


class Prog:
    ENGS = ("sp", "act", "pool", "dve", "pe")

    def __init__(self, nc, es, n_dma_sems=80):
        self.nc = nc
        self.h = {"sp": nc.sync, "act": nc.scalar, "pool": nc.gpsimd, "dve": nc.vector, "pe": nc.tensor}
        self.sem = {e: es.enter_context(nc.semaphore("c_" + e)) for e in self.ENGS if e != "sp"}
        self.cnt = {e: 0 for e in self.sem}
        self.free_dma = [es.enter_context(nc.semaphore("d%d" % i)) for i in range(n_dma_sems)]
        self.dma_sem = {}
        self.dma_cnt = {}
        self.ops = {e: [] for e in self.ENGS}
        self.lw = {}
        self.rd = {}
        self.barrier = {}
        self.n_ops = 0

    def op(self, eng, fn, reads=(), writes=(), dma=None):
        o = _Op(eng, fn, dma)
        if dma is not None and dma not in self.dma_sem:
            self.dma_sem[dma] = self.free_dma.pop()
            self.dma_cnt[dma] = 0
        deps = []
        for k in reads:
            w = self.lw.get(k)
            if w is not None:
                deps.append(w)
        for k in writes:
            w = self.lw.get(k)
            if w is not None:
                deps.append(w)
            deps.extend(self.rd.get(k, ()))
        for d in deps:
            if d is o:
                continue
            if d.dma is None and d.eng == eng and eng == "pe":
                continue
            o.deps.append(d)
            d.inc = True
        for k in writes:
            self.lw[k] = o
            self.rd[k] = []
        for k in reads:
            self.rd.setdefault(k, []).append(o)
        self.ops[eng].append(o)
        self.n_ops += 1
        return o

    def flush(self, final=False):
        nc = self.nc
        for e in self.ENGS:
            ops = self.ops[e]
            last_compute = None
            for o in ops:
                if o.dma is None:
                    last_compute = o
            if last_compute is not None:
                last_compute.inc = True
            for o in ops:
                if o.dma is not None:
                    self.dma_cnt[o.dma] += 16
                    o.val = self.dma_cnt[o.dma]
                elif o.inc:
                    self.cnt[e] += 1
                    o.val = self.cnt[e]
        barrier = dict(self.barrier)
        ops_all = self.ops
        sems = self.sem
        dma_sem = self.dma_sem

        def emit(e, eng):
            waited = {}

            def wait(sem, val):
                if waited.get(id(sem), 0) >= val:
                    return
                waited[id(sem)] = val
                eng.wait_ge(sem, val)

            if ops_all[e] or final:
                for sem, val in barrier.values():
                    if val > 0:
                        wait(sem, val)
            for o in ops_all[e]:
                need = {}
                for d in o.deps:
                    sem = dma_sem[d.dma] if d.dma is not None else sems[d.eng]
                    if need.get(id(sem), (None, 0))[1] < d.val:
                        need[id(sem)] = (sem, d.val)
                for sem, val in need.values():
                    wait(sem, val)
                ins = o.fn(eng)
                if o.dma is not None:
                    ins.then_inc(dma_sem[o.dma], 16)
                elif o.inc:
                    ins.then_inc(sems[e], 1)

        with nc.Block() as block:
            @block.sync
            def _(eng):
                emit("sp", eng)

            @block.scalar
            def _(eng):
                emit("act", eng)

            @block.gpsimd
            def _(eng):
                emit("pool", eng)

            @block.vector
            def _(eng):
                emit("dve", eng)

            @block.tensor
            def _(eng):
                emit("pe", eng)

        self.barrier = {}
        for e, sem in self.sem.items():
            self.barrier[id(sem)] = (sem, self.cnt[e])
        for k, sem in self.dma_sem.items():
            self.barrier[id(sem)] = (sem, self.dma_cnt[k])
        self.ops = {e: [] for e in self.ENGS}
        self.lw = {}
        self.rd = {}


def build_program(stop_after=None, dbg=None):
    nc = bass.Bass("TRN2", target_bir_lowering=False)
    dt = nc.dram_tensor
    x_d = dt("x", [S, D], F32, kind="ExternalInput").ap()
    poolw_d = dt("pool_w", [4, 256, 256], F32, kind="ExternalInput").ap()
    pscale_d = dt("pool_scale", [1, D], F32, kind="ExternalInput").ap()
    lnp_d = dt("lnp", [8, D], F32, kind="ExternalInput").ap()
    w1_d = dt("mlp_w1", [2, D, DFF], F32, kind="ExternalInput").ap()
    w2_d = dt("mlp_w2", [2, DFF, D], F32, kind="ExternalInput").ap()
    band_d = dt("c_band", [128, 1536], F32, kind="ExternalInput").ap()
    ident_d = dt("c_ident", [128, 128], F32, kind="ExternalInput").ap()
    wattn_d = dt("wattn", [8, 128, 3 * NC_ * 128], F32, kind="ExternalInput").ap()
    perm_d = dt("c_perm", [128, 128], F32, kind="ExternalInput").ap()
    wo_d = dt("w_o", [D, D], F32, kind="ExternalInput").ap()
    rope_d = dt("c_rope", [128, 2, S], F32, kind="ExternalInput").ap()
    cb_d = dt("c_cb", [128, 512], F32, kind="ExternalInput").ap()
    indk_d = dt("c_indk", [NBLK, S], F32, kind="ExternalInput").ap()
    summ_d = dt("c_summ", [64, 4 * 80], F32, kind="ExternalInput").ap()
    out_d = dt("out", [S, D], F32, kind="ExternalOutput").ap()

    with ExitStack() as es:
        P = Prog(nc, es)
        sb = lambda name, shape, dtype: es.enter_context(nc.sbuf_tensor(name, shape, dtype))
        xres = sb("xres", [128, NT, D], F32)
        xT = sb("xT", [128, NC_, S], BF16)
        lng = sb("lng", [128, D], F32)
        lnb = sb("lnb", [128, D], F32)
        ident = sb("ident", [128, 128], BF16)
        stats = sb("stats", [128, 4, 12], F32)
        mv = sb("mv", [128, 4, 2], F32)
        rstd = sb("rstd", [128, 4, 1], F32)
        ctx = {}

        def load_ln_params(idx):
            P.op("sp", lambda e: e.dma_start(out=lng[:], in_=lnp_d[2 * idx:2 * idx + 1, :].partition_broadcast(128)),
                 writes=["lng"], dma="lng")
            P.op("sp", lambda e: e.dma_start(out=lnb[:], in_=lnp_d[2 * idx + 1:2 * idx + 2, :].partition_broadcast(128)),
                 writes=["lnb"], dma="lnb")

        def xk(T):
            return [("xres", T, 0), ("xres", T, 1)]

        def ln_s1(T):
            j = T % 4
            xr = xres[:, T, :]
            P.op("dve", lambda e: e.bn_stats(stats[:, j, 0:6], xr[:, 0:512]),
                 reads=[("xres", T, 0)], writes=[("stats", j, 0)])
            P.op("dve", lambda e: e.bn_stats(stats[:, j, 6:12], xr[:, 512:1024]),
                 reads=[("xres", T, 1)], writes=[("stats", j, 1)])
            P.op("dve", lambda e: e.bn_aggr(mv[:, j, :], stats[:, j, :]),
                 reads=[("stats", j, 0), ("stats", j, 1)], writes=[("mv", j)])
            P.op("act", lambda e: e.activation(out=rstd[:, j, :], in_=mv[:, j, 1:2], func=AF.Sqrt, bias=EPS, scale=1.0),
                 reads=[("mv", j)], writes=[("rstd", j)])
            P.op("dve", lambda e: e.reciprocal(out=rstd[:, j, :], in_=rstd[:, j, :]),
                 reads=[("rstd", j)], writes=[("rstd", j)])
            P.op("dve", lambda e: e.tensor_scalar(out=xr, in0=xr, scalar1=mv[:, j, 0:1], scalar2=rstd[:, j, :],
                                                  op0=ALU.subtract, op1=ALU.mult),
                 reads=xk(T) + [("mv", j), ("rstd", j)], writes=xk(T))

        def ln_s2(T, final):
            j = T % 4
            jb = T % ctx.get("ring", 4)
            xr = xres[:, T, :]
            P.op("pool", lambda e: e.tensor_tensor(out=xr, in0=xr, in1=lng[:], op=ALU.mult),
                 reads=xk(T) + ["lng"], writes=xk(T))
            P.op("pool", lambda e: e.tensor_tensor(out=xr, in0=xr, in1=lnb[:], op=ALU.add),
                 reads=xk(T) + ["lnb"], writes=xk(T))
            if final:
                P.op("sp", lambda e: e.dma_start(out=out_d[T * 128:(T + 1) * 128, :], in_=xr),
                     reads=xk(T), dma=("out", T % 4))
                return
            xb16 = ctx["xb16"]
            P.op("act", lambda e: e.activation(out=xb16[:, jb, :], in_=xr, func=AF.Copy),
                 reads=xk(T), writes=[("xb16", jb)])

        def pipeline(n, stages, oldest_first=False):
            maxs = max(sk for sk, _ in stages)
            for step in range(n + maxs):
                for sk, fn in (sorted(stages, key=lambda st: -st[0]) if oldest_first else stages):
                    T = step - sk
                    if 0 <= T < n:
                        fn(T)

        def ln_b(T, tp):
            jb = T % ctx.get("ring", 4)
            xb16 = ctx["xb16"]
            for c in range(NC_):
                P.op("pe", lambda e, c=c: e.transpose(tp[:, c, :], xb16[:, jb, c * 128:(c + 1) * 128], ident[:]),
                     reads=[("xb16", jb), "ident"], writes=["tp"])
            P.op("act", lambda e: e.activation(out=xT[:, :, T * 128:(T + 1) * 128], in_=tp[:, :, :], func=AF.Copy),
                 reads=["tp"], writes=[("xT", T)])

        def mlp_phase(l, ln_idx, final):
            with ExitStack() as ps:
                sbp = lambda name, shape, dtype: ps.enter_context(nc.sbuf_tensor(name + "_m%d" % l, shape, dtype))
                pp = lambda name, shape, dtype: ps.enter_context(nc.psum_tensor(name + "_m%d" % l, shape, dtype))
                wq1 = sbp("wq1", [128, 2, NC_, 1024], BF16)
                wq2 = sbp("wq2", [128, 2, 8, D], BF16)
                hT = sbp("hT", [128, 2, 8, 512], BF16)
                rl = sbp("rl", [128, 2, 512], F32)
                ps_h = pp("ps_h", [128, 2, 512], F32)
                ps_o = pp("ps_o", [128, 4, 512], F32)
                tp = pp("tp", [128, NC_, 128], BF16)
                ctx["xb16"] = sbp("xb16", [128, 8, D], BF16)
                ctx["ring"] = 8
                w1v = w1_d[l].rearrange("(dc p) f -> p dc f", p=128)

                def load_q(q):
                    slot = q % 2
                    for g in range(4):
                        P.op("pool", lambda e, g=g: e.dma_start(
                            out=wq1[:, slot, :, g * 256:(g + 1) * 256],
                            in_=w1v[:, :, q * 1024 + g * 256:q * 1024 + (g + 1) * 256]),
                            writes=[("wq1", slot, g)], dma=("wq1", slot, g))
                    for g in range(4):
                        P.op("pool", lambda e, g=g: e.dma_start(
                            out=wq2[:, slot, 2 * g:2 * g + 2, :],
                            in_=w2_d[l, q * 1024 + g * 256:q * 1024 + (g + 1) * 256, :].rearrange(
                                "(fc p) d -> p fc d", p=128)),
                            writes=[("wq2", slot, g)], dma=("wq2", slot, g))

                load_ln_params(ln_idx)
                load_q(0)
                pending_b = []
                pending_s2 = []
                for q in range(4):
                    slot = q % 2
                    if q + 1 < 4:
                        load_q(q + 1)
                    for tg in range(4):
                        hb = (q * 4 + tg) % 2
                        xT_keys = [("xT", tg * 4 + i) for i in range(4)]
                        for fc in range(8):
                            bank = fc % 2
                            for dc in range(NC_):
                                P.op("pe", lambda e, fc=fc, dc=dc, bank=bank, slot=slot, tg=tg: e.matmul(
                                    ps_h[:, bank, :], wq1[:, slot, dc, fc * 128:(fc + 1) * 128],
                                    xT[:, dc, tg * 512:(tg + 1) * 512], start=(dc == 0), stop=(dc == NC_ - 1)),
                                    reads=[("wq1", slot, fc // 2)] + xT_keys, writes=[("ps_h", bank)])
                            P.op("act", lambda e, bank=bank: e.activation(out=rl[:, bank, :], in_=ps_h[:, bank, :],
                                                                           func=AF.Relu),
                                 reads=[("ps_h", bank)], writes=[("rl", bank)])
                            if fc > 0:
                                P.op("act", lambda e, fc=fc, bank=bank, hb=hb: e.activation(
                                    out=hT[:, hb, fc - 1, :], in_=rl[:, 1 - bank, :], func=AF.Square),
                                    reads=[("rl", 1 - bank)], writes=[("hT", hb, fc - 1)])
                        P.op("act", lambda e, hb=hb: e.activation(out=hT[:, hb, 7, :], in_=rl[:, 1, :], func=AF.Square),
                             reads=[("rl", 1)], writes=[("hT", hb, 7)])
                        for tt in range(4):
                            T = tg * 4 + tt
                            for half in range(2):
                                ob = (tt * 2 + half) % 4
                                for fc in range(8):
                                    P.op("pe", lambda e, fc=fc, tt=tt, half=half, ob=ob, hb=hb, slot=slot: e.matmul(
                                        ps_o[:, ob, :], hT[:, hb, fc, tt * 128:(tt + 1) * 128],
                                        wq2[:, slot, fc, half * 512:(half + 1) * 512], start=(fc == 0), stop=(fc == 7)),
                                        reads=[("hT", hb, fc), ("wq2", slot, fc // 2)], writes=[("ps_o", ob)])
                                xs = xres[:, T, half * 512:(half + 1) * 512]
                                if q == 0:
                                    P.op("dve", lambda e, xs=xs, ob=ob: e.scalar_tensor_tensor(
                                        out=xs, in0=xs, scalar=ALPHA, in1=ps_o[:, ob, :], op0=ALU.mult, op1=ALU.add),
                                        reads=[("xres", T, half), ("ps_o", ob)], writes=[("xres", T, half)])
                                else:
                                    P.op("dve", lambda e, xs=xs, ob=ob: e.tensor_tensor(
                                        out=xs, in0=xs, in1=ps_o[:, ob, :], op=ALU.add),
                                        reads=[("xres", T, half), ("ps_o", ob)], writes=[("xres", T, half)])
                            if q == 3:
                                ln_s1(T)
                                for T2 in pending_s2:
                                    ln_s2(T2, final)
                                    if not final:
                                        pending_b.append(T2)
                                pending_s2 = [T]
                                if pending_b and pending_b[0] < tg * 4:
                                    ln_b(pending_b.pop(0), tp)
                for T in pending_b:
                    ln_b(T, tp)
                for T2 in pending_s2:
                    ln_s2(T2, final)
                    if not final:
                        ln_b(T2, tp)
                if stop_after is not None and stop_after == ln_idx:
                    return True
                P.flush()
            return False

        def attn_phase():
            with ExitStack() as po:
                OT = po.enter_context(nc.sbuf_tensor("OT", [128, NC_, S], BF16))
                with ExitStack() as ps:
                    sbp = lambda name, shape, dtype: ps.enter_context(nc.sbuf_tensor(name, shape, dtype))
                    pp = lambda name, shape, dtype: ps.enter_context(nc.psum_tensor(name, shape, dtype))
                    wpair = sbp("wpair", [128, 3, NC_, 128], BF16)
                    permm = sbp("permm", [128, 128], BF16)
                    qb = sbp("qb", [128, 2, 512], BF16)
                    Qz = sbp("Qz", [128, 2, S], BF16)
                    Kz = sbp("Kz", [128, 2, S], BF16)
                    Vp = sbp("Vp", [128, NT, 2, 128], BF16)
                    rope = sbp("rope", [128, 2, 2, 512], F32)
                    t1 = sbp("t1", [128, 2, 512], F32)
                    t2 = sbp("t2", [128, 2, 512], F32)
                    PT = sbp("PT", [128, 6, 512], BF16)
                    kms = sbp("kms", [128, NBLK], F32)
                    KMd = sbp("KMd", [128, 128], BF16)
                    G = sbp("G", [64, 512], BF16)
                    CB = sbp("CB", [128, 2, 256], BF16)
                    SUMM = sbp("SUMM", [64, 4, 80], BF16)
                    den = sbp("den", [128, 2, 512], F32)
                    NWB = 5
                    ps_proj = pp("ps_proj", [128, NWB, 512], F32)
                    ps_s = ps_proj
                    ps_o = pp("ps_o2", [128, 2, 512], F32)
                    ps_sel = pp("ps_sel", [128, 512], F32)

                    P.op("pool", lambda e: e.dma_start(out=CB[:].rearrange("p a b -> p (a b)"), in_=cb_d[:, :]),
                         writes=["CB"], dma="CB")
                    P.op("pool", lambda e: e.memset(Qz[:].rearrange("p a b -> p (a b)"), 0.0), writes=["Qz0"])
                    P.op("pool", lambda e: e.memset(Kz[:].rearrange("p a b -> p (a b)"), 0.0), writes=["Kz0"])
                    P.op("pool", lambda e: e.dma_start(out=Kz[64:72, 0, :], in_=indk_d[:, :]),
                         reads=["Kz0"], writes=["Kzi0"], dma="Kzi0")
                    P.op("pool", lambda e: e.dma_start(out=Kz[0:8, 1, :], in_=indk_d[:, :]),
                         reads=["Kz0"], writes=["Kzi1"], dma="Kzi1")
                    P.op("pool", lambda e: e.dma_start(out=SUMM[:].rearrange("p a b -> p (a b)"), in_=summ_d[:, :]),
                         writes=["SUMM"], dma="SUMM")
                    P.op("pool", lambda e: e.dma_start(out=permm[:], in_=perm_d[:, :]), writes=["permm"], dma="permm")
                    P.op("pool", lambda e: e.memset(Vp[:, :, 0, 64:128], 1.0), writes=["Vp1a"])
                    P.op("pool", lambda e: e.memset(Vp[:, :, 1, 0:64], 1.0), writes=["Vp1b"])
                    P.op("pool", lambda e: e.memset(KMd[:], 0.0), writes=["KMd0"])

                    cnt = {"pj": 0, "t": 0, "s": 0, "o": 0, "rope": 0}

                    def next_bank():
                        b = cnt["pj"] % NWB
                        cnt["pj"] += 1
                        return b

                    def qk_item(c, tg):
                        rs = cnt["rope"] % 2
                        cnt["rope"] += 1
                        xT_keys = [("xT", tg * 4 + i) for i in range(4)]
                        P.op("sp", lambda e: e.dma_start(
                            out=rope[:, rs, :, :], in_=rope_d[:, :, tg * 512:(tg + 1) * 512]),
                            writes=[("rope", rs)], dma=("rope", rs))
                        banks = []
                        for which in (0, 1):
                            ba = next_bank()
                            banks.append(ba)
                            for dc in range(NC_):
                                P.op("pe", lambda e, which=which, ba=ba, dc=dc: e.matmul(
                                    ps_proj[:, ba, :], wpair[:, which, dc, :], xT[:, dc, tg * 512:(tg + 1) * 512],
                                    start=(dc == 0), stop=(dc == NC_ - 1)),
                                    reads=["wpair"] + xT_keys, writes=[("pp", ba)])
                            P.op("act", lambda e, which=which, ba=ba: e.activation(
                                out=qb[:, which, :], in_=ps_proj[:, ba, :], func=AF.Copy),
                                reads=[("pp", ba)], writes=[("qb", which)])
                        for which in (0, 1):
                            ba = banks[which]
                            bb = next_bank()
                            P.op("pe", lambda e, which=which, bb=bb: e.matmul(
                                ps_proj[:, bb, :], permm[:], qb[:, which, :], start=True, stop=True),
                                reads=["permm", ("qb", which)], writes=[("pp", bb)])
                            ts = cnt["t"] % 2
                            cnt["t"] += 1
                            P.op("dve", lambda e, ts=ts, ba=ba: e.tensor_tensor(
                                out=t1[:, ts, :], in0=ps_proj[:, ba, :], in1=rope[:, rs, 0, :], op=ALU.mult),
                                reads=[("pp", ba), ("rope", rs)], writes=[("t1", ts)])
                            P.op("dve", lambda e, ts=ts, bb=bb: e.tensor_tensor(
                                out=t2[:, ts, :], in0=ps_proj[:, bb, :], in1=rope[:, rs, 1, :], op=ALU.mult),
                                reads=[("pp", bb), ("rope", rs)], writes=[("t2", ts)])
                            dstz, dkey, zkey = (Qz, "Qp", "Qz0") if which == 0 else (Kz, "Kp", "Kz0")
                            for hh in (0, 1):
                                rr = slice(hh * 64, hh * 64 + 64)
                                P.op("dve", lambda e, ts=ts, hh=hh, rr=rr, dstz=dstz: e.tensor_tensor(
                                    out=dstz[rr, hh, tg * 512:(tg + 1) * 512], in0=t1[rr, ts, :], in1=t2[rr, ts, :],
                                    op=ALU.add),
                                    reads=[("t1", ts), ("t2", ts), zkey], writes=[(dkey, hh, tg)])
                                if which == 1:
                                    P.op("dve", lambda e, hh=hh, rr=rr: e.tensor_reduce(
                                        out=kms[rr, 2 * tg:2 * tg + 2],
                                        in_=Kz[rr, hh, tg * 512:(tg + 1) * 512].rearrange("p (b k) -> p b k", b=2),
                                        axis=AX.X, op=ALU.add),
                                        reads=[("Kp", hh, tg)], writes=[("kms", hh, tg)])

                    def v_item(c, g4):
                        bv = next_bank()
                        for i in range(4):
                            T = g4 * 4 + i
                            for dc in range(NC_):
                                P.op("pe", lambda e, i=i, T=T, dc=dc: e.matmul(
                                    ps_proj[:, bv, i * 128:(i + 1) * 128], xT[:, dc, T * 128:(T + 1) * 128],
                                    wpair[:, 2, dc, :], start=(dc == 0), stop=(dc == NC_ - 1)),
                                    reads=["wpair", ("xT", T)], writes=[("pp", bv)])
                        pv = ps_proj[:, bv, :].rearrange("p (a b) -> p a b", a=4)
                        P.op("dve", lambda e: e.tensor_copy(
                            out=Vp[:, g4 * 4:(g4 + 1) * 4, 0, 0:64], in_=pv[:, :, 0:64]),
                            reads=[("pp", bv)], writes=[("Vp", g4, 0)])
                        P.op("dve", lambda e: e.tensor_copy(
                            out=Vp[:, g4 * 4:(g4 + 1) * 4, 1, 64:128], in_=pv[:, :, 64:128]),
                            reads=[("pp", bv)], writes=[("Vp", g4, 1)])

                    def kmd_item(c):
                        for hh in (0, 1):
                            r0 = hh * 64
                            P.op("dve", lambda e, r0=r0: e.tensor_tensor(
                                out=KMd[r0:r0 + 64, r0:r0 + 64].rearrange("p (n m) -> p n m", n=NBLK),
                                in0=kms[r0:r0 + 64, :].unsqueeze(1).broadcast_to([64, NBLK, NBLK]),
                                in1=kms[r0:r0 + 64, :].unsqueeze(2).broadcast_to([64, NBLK, NBLK]),
                                op=ALU.subtract),
                                reads=[("kms", hh, i) for i in range(4)] + ["KMd0"], writes=[("KMd", hh)])

                    def sel_a(c, own):
                        jq = own // 2
                        for hh in (0, 1):
                            P.op("pe", lambda e, hh=hh: e.matmul(
                                ps_sel[0:64, hh * 256:(hh + 1) * 256], KMd[:, hh * 64:(hh + 1) * 64],
                                Qz[:, hh, own * 256:(own + 1) * 256], start=True, stop=True),
                                reads=[("KMd", hh), "KMd0", ("Qp", hh, jq), "Qz0", ("Qzb", hh, own)], writes=["selbank"])
                        P.op("dve", lambda e: e.tensor_single_scalar(out=G[:], in_=ps_sel[0:64, :], scalar=0.0,
                                                                     op=ALU.is_gt),
                             reads=["selbank"], writes=["G"])

                    def sel_b(c, own):
                        P.op("pe", lambda e: e.matmul(
                            ps_sel[0:72, 0:256], SUMM[:, own - 4, 0:72], G[:, 0:256], start=True, stop=True),
                            reads=["G", "SUMM", "selbank"], writes=["selbank"])
                        P.op("pe", lambda e: e.matmul(
                            ps_sel[0:8, 256:512], SUMM[:, own - 4, 72:80], G[:, 256:512], start=True, stop=True),
                            reads=["G", "SUMM", "selbank"], writes=["selbank"])
                        P.op("dve", lambda e: e.tensor_scalar(
                            out=Qz[64:72, 0, own * 256:(own + 1) * 256], in0=ps_sel[64:72, 0:256], scalar1=2.5,
                            scalar2=NEG, op0=ALU.is_ge, op1=ALU.mult),
                            reads=["selbank", "Qz0"], writes=[("Qzb", 0, own)])
                        P.op("dve", lambda e: e.tensor_scalar(
                            out=Qz[0:8, 1, own * 256:(own + 1) * 256], in0=ps_sel[0:8, 256:512], scalar1=2.5,
                            scalar2=NEG, op0=ALU.is_ge, op1=ALU.mult),
                            reads=["selbank", "Qz0"], writes=[("Qzb", 1, own)])

                    def emit_s(c, it):
                        hh, j, kt, kind = it
                        wbk = next_bank()
                        pslot = cnt["s"] % 6
                        cnt["s"] += 1
                        kap = Kz[:, hh, kt * 128:(kt + 1) * 128]
                        kk = [("Kp", hh, kt // 4), "Kz0", "Kzi0", "Kzi1", ("Qp", hh, j), "Qz0"]
                        kk += [("Qzb", hh, o) for o in (2 * j, 2 * j + 1) if o >= 4]
                        if kind == "diagB":
                            cols = slice(256, 512)
                            P.op("pe", lambda e: e.matmul(
                                ps_s[:, wbk, 256:512], kap, Qz[:, hh, j * 512 + 256:(j + 1) * 512],
                                start=True, stop=False), reads=kk, writes=[("pp", wbk)])
                            P.op("pe", lambda e: e.matmul(
                                ps_s[:, wbk, 256:512], ident[:], CB[:, kt % 2, :], start=False, stop=True),
                                reads=["ident", "CB"], writes=[("pp", wbk)])
                        else:
                            cols = slice(0, 512)
                            P.op("pe", lambda e: e.matmul(
                                ps_s[:, wbk, :], kap, Qz[:, hh, j * 512:(j + 1) * 512], start=True,
                                stop=(kind == "full")), reads=kk, writes=[("pp", wbk)])
                            if kind == "mixed":
                                P.op("pe", lambda e: e.matmul(
                                    ps_s[:, wbk, 0:256], ident[:], CB[:, kt % 2, :], start=False, stop=True),
                                    reads=["ident", "CB"], writes=[("pp", wbk)])
                        P.op("act", lambda e: e.activation(out=PT[:, pslot, cols], in_=ps_s[:, wbk, cols],
                                                           func=AF.Exp, scale=DH ** -0.5),
                             reads=[("pp", wbk)], writes=[("PT", pslot)])
                        return pslot

                    def emit_pv(c, it, pslot, ob):
                        hh, j, kt, kind = it
                        cols = slice(256, 512) if kind == "diagB" else slice(0, 512)
                        first = (kt == 0)
                        last = (kt == 4 * j + 3)
                        P.op("pe", lambda e: e.matmul(
                            ps_o[:, ob, cols], Vp[:, kt, hh, :], PT[:, pslot, cols], start=first, stop=last),
                            reads=[("Vp", kt // 4, hh), "Vp1a", "Vp1b", ("PT", pslot)], writes=[("ps_o", ob)])
                        if last:
                            orow = slice(0, 64) if hh == 0 else slice(64, 128)
                            drow = slice(64, 128) if hh == 0 else slice(0, 64)
                            P.op("dve", lambda e: e.reciprocal(out=den[drow, ob, :], in_=ps_o[drow, ob, :]),
                                 reads=[("ps_o", ob)], writes=[("den", ob)])
                            P.op("dve", lambda e: e.tensor_tensor(
                                out=OT[orow, c, j * 512:(j + 1) * 512], in0=ps_o[orow, ob, :],
                                in1=den[drow, ob, :], op=ALU.mult),
                                reads=[("ps_o", ob), ("den", ob)], writes=[("OT", c, hh, j)])

                    DEPTH = 4
                    for c in range(NC_):
                        P.op("pool", lambda e, c=c: e.dma_start(
                            out=wpair[:].rearrange("p a b c -> p (a b c)"), in_=wattn_d[c]),
                            writes=["wpair"], dma="wpair")
                        for tg in range(4):
                            qk_item(c, tg)
                        for g4 in range(4):
                            v_item(c, g4)
                        items = []
                        for hh in (0, 1):
                            for j in range(4):
                                items += [(hh, j, kt, "full") for kt in range(4 * j)]
                                items += [(hh, j, kt, "mixed") for kt in (4 * j, 4 * j + 1)]
                                items += [(hh, j, kt, "diagB") for kt in (4 * j + 2, 4 * j + 3)]
                        hooks = {
                            0: [lambda c=c: kmd_item(c), lambda c=c: sel_a(c, 4)],
                            2: [lambda c=c: sel_b(c, 4), lambda c=c: sel_a(c, 5)],
                            4: [lambda c=c: sel_b(c, 5), lambda c=c: sel_a(c, 6)],
                            6: [lambda c=c: sel_b(c, 6), lambda c=c: sel_a(c, 7)],
                            8: [lambda c=c: sel_b(c, 7)],
                        }
                        gid = {}
                        for it in items:
                            if (it[0], it[1]) not in gid:
                                gid[(it[0], it[1])] = cnt["o"]
                                cnt["o"] += 1
                        pslots = {}
                        for i in range(len(items) + DEPTH):
                            for f in hooks.get(i, ()):
                                f()
                            if i < len(items):
                                pslots[i] = emit_s(c, items[i])
                            if i >= DEPTH:
                                it = items[i - DEPTH]
                                emit_pv(c, it, pslots[i - DEPTH], gid[(it[0], it[1])] % 2)
                    P.flush()
                with ExitStack() as ps:
                    sbp = lambda name, shape, dtype: ps.enter_context(nc.sbuf_tensor(name, shape, dtype))
                    pp = lambda name, shape, dtype: ps.enter_context(nc.psum_tensor(name, shape, dtype))
                    wo = sbp("wo", [128, NC_, D], BF16)
                    ctx["xb16"] = sbp("xb16_a", [128, 4, D], BF16)
                    ctx["ring"] = 4
                    ps_y = pp("ps_y2", [128, 2, 512], F32)
                    tp = pp("tp_a", [128, NC_, 128], BF16)
                    for g in range(4):
                        P.op("pool", lambda e, g=g: e.dma_start(
                            out=wo[:, 2 * g:2 * g + 2, :],
                            in_=wo_d[g * 256:(g + 1) * 256, :].rearrange("(c p) e -> p c e", p=128)),
                            writes=[("wo", g)], dma=("wo", g))
                    load_ln_params(2)

                    def wo_s0(T):
                        for half in range(2):
                            ob = (T * 2 + half) % 2
                            for c in range(NC_):
                                P.op("pe", lambda e, half=half, ob=ob, c=c: e.matmul(
                                    ps_y[:, ob, :], OT[:, c, T * 128:(T + 1) * 128],
                                    wo[:, c, half * 512:(half + 1) * 512], start=(c == 0), stop=(c == NC_ - 1)),
                                    reads=[("wo", c // 2)], writes=[("ps_y", ob)])
                            xs = xres[:, T, half * 512:(half + 1) * 512]
                            P.op("dve", lambda e, xs=xs, ob=ob: e.scalar_tensor_tensor(
                                out=xs, in0=xs, scalar=ALPHA, in1=ps_y[:, ob, :], op0=ALU.mult, op1=ALU.add),
                                reads=[("xres", T, half), ("ps_y", ob)], writes=[("xres", T, half)])

                    pipeline(NT, [(0, wo_s0), (2, ln_s1), (4, lambda T: ln_s2(T, False)), (6, lambda T: ln_b(T, tp))])
                    if stop_after == 2:
                        return True
                    P.flush()
            return False

        def dump_and_finish():
            for T in range(NT):
                P.op("sp", lambda e, T=T: e.dma_start(out=out_d[T * 128:(T + 1) * 128, :], in_=xres[:, T, :]),
                     reads=xk(T), dma=("out", T % 4))
            P.flush()
            P.flush(final=True)
            return nc

        with ExitStack() as ps:
            sbp = lambda name, shape, dtype: ps.enter_context(nc.sbuf_tensor(name, shape, dtype))
            pp = lambda name, shape, dtype: ps.enter_context(nc.psum_tensor(name, shape, dtype))
            xbf = sbp("xbf", [128, 3, D], BF16)
            pooled = sbp("pooled", [128, 2, NC_, 128], BF16)
            band = sbp("band", [128, 1536], BF16)
            pw_stage = sbp("pw_stage", [128, 4, 2, 256], F32)
            sc_tab = sbp("sc_tab", [128, D], F32)
            poolw = sbp("poolw", [128, 4, 2, 256], BF16)
            ps_pool = pp("ps_pool", [128, NC_, 128], F32)
            ps_y = pp("ps_y", [128, 2, D], F32)
            tp = pp("tp", [128, NC_, 128], BF16)
            ctx["xb16"] = sbp("xb16_p0", [128, 4, D], BF16)
            ctx["ring"] = 4

            load_ln_params(0)
            P.op("sp", lambda e: e.dma_start(out=sc_tab[:], in_=pscale_d[0:1, :].partition_broadcast(128)),
                 writes=["sc_tab"], dma="sc_tab")
            P.op("sp", lambda e: e.dma_start(out=pw_stage[:], in_=poolw_d.rearrange("g (k p) e -> p g k e", p=128)),
                 writes=["pw_stage"], dma="pw_stage")
            P.op("pool", lambda e: e.dma_start(out=band[:], in_=band_d[:, :]), writes=["band"], dma="band")
            P.op("pool", lambda e: e.dma_start(out=ident[:], in_=ident_d[:, :]), writes=["ident"], dma="ident")
            for k in range(2):
                P.op("dve", lambda e, k=k: e.tensor_tensor(
                    out=poolw[:, :, k, :], in0=pw_stage[:, :, k, :],
                    in1=sc_tab[:].rearrange("p (g e) -> p g e", g=4), op=ALU.mult),
                    reads=["pw_stage", "sc_tab"], writes=[("poolw", k)])
            for T in range(NT):
                P.op("sp", lambda e, T=T: e.dma_start(out=xres[:, T, :], in_=x_d[T * 128:(T + 1) * 128, :]),
                     writes=xk(T), dma=("xld", T))

            def load_xbf(T):
                P.op("pool", lambda e, T=T: e.dma_start(out=xbf[:, T % 3, :], in_=x_d[T * 128:(T + 1) * 128, :]),
                     writes=[("xbf", T % 3)], dma=("xbf", T % 3))

            load_xbf(0)
            load_xbf(1)

            def p0_s0(T):
                for c in range(NC_):
                    g = c // 2
                    boff = (512 if T == 0 else 0) + g * 128
                    P.op("pe", lambda e, c=c, boff=boff: e.matmul(
                        ps_pool[:, c, :], xbf[:, T % 3, c * 128:(c + 1) * 128], band[:, boff:boff + 128],
                        start=True, stop=(T == 0)),
                        reads=[("xbf", T % 3), "band"], writes=["ps_pool"])
                    if T > 0:
                        P.op("pe", lambda e, c=c, g=g: e.matmul(
                            ps_pool[:, c, :], xbf[64:128, (T - 1) % 3, c * 128:(c + 1) * 128],
                            band[64:128, 1024 + g * 128:1024 + (g + 1) * 128], start=False, stop=True),
                            reads=[("xbf", (T - 1) % 3), "band"], writes=["ps_pool"])
                if T + 2 < NT:
                    load_xbf(T + 2)
                P.op("act", lambda e: e.activation(out=pooled[:, T % 2, :, :], in_=ps_pool[:, :, :], func=AF.Copy),
                     reads=["ps_pool"], writes=[("pooled", T % 2)])
                for g in range(4):
                    for k in range(2):
                        P.op("pe", lambda e, g=g, k=k: e.matmul(
                            ps_y[:, T % 2, g * 256:(g + 1) * 256], pooled[:, T % 2, 2 * g + k, :], poolw[:, g, k, :],
                            start=(k == 0), stop=(k == 1)),
                            reads=[("pooled", T % 2), ("poolw", k)], writes=[("ps_y", T % 2)])
                P.op("dve", lambda e: e.scalar_tensor_tensor(
                    out=xres[:, T, :], in0=xres[:, T, :], scalar=ALPHA, in1=ps_y[:, T % 2, :],
                    op0=ALU.mult, op1=ALU.add),
                    reads=xk(T) + [("ps_y", T % 2)], writes=xk(T))

            if dbg == 'noln':
                pipeline(NT, [(0, p0_s0)])
            else:
                pipeline(NT, [(0, p0_s0), (2, ln_s1), (4, lambda T: ln_s2(T, False)), (6, lambda T: ln_b(T, tp))],
                         oldest_first=True)
            if stop_after == 0:
                return dump_and_finish()
            P.flush()

        mlp_phase(0, 1, False)
        if stop_after == 1:
            return dump_and_finish()
        attn_phase()
        if stop_after == 2:
            return dump_and_finish()
        mlp_phase(1, 3, True)
        P.flush(final=True)
    return nc


def _consts():
    band = np.zeros((128, 1536), np.float32)
    tl = np.arange(128)
    for g, w in enumerate((2, 4, 8, 16)):
        dd = tl[None, :] - tl[:, None]
        inwin = (dd >= 0) & (dd < w)
        Dg = inwin.astype(np.float32) / w - np.eye(128, dtype=np.float32)
        cnt = np.minimum(tl + 1, w).astype(np.float32)
        D0 = inwin.astype(np.float32) / cnt[None, :] - np.eye(128, dtype=np.float32)
        dd2 = tl[None, :] + 128 - tl[:, None]
        U = ((dd2 >= 0) & (dd2 < w)).astype(np.float32) / w
        band[:, g * 128:(g + 1) * 128] = Dg
        band[:, 512 + g * 128:512 + (g + 1) * 128] = D0
        band[:, 1024 + g * 128:1024 + (g + 1) * 128] = U
    half = DH // 2
    inv = (np.float32(10000.0) ** (-np.arange(half, dtype=np.float32) * np.float32(2.0) / np.float32(DH))).astype(np.float32)
    ang = (np.arange(S, dtype=np.float32)[None, :] * inv[:, None]).astype(np.float32)
    cos = np.cos(ang).astype(np.float32)
    sin = np.sin(ang).astype(np.float32)
    rope = np.zeros((128, 2, S), np.float32)
    for p in range(128):
        rope[p, 0] = cos[p % 32]
        rope[p, 1] = -sin[p % 32] if (p % 64) < 32 else sin[p % 32]
    kk = np.arange(128)[:, None]
    qq = np.arange(256)[None, :]
    cb = np.zeros((128, 2, 256), np.float32)
    cb[:, 0] = np.where(kk > qq, NEG, 0.0)
    cb[:, 1] = np.where(kk + 128 > qq, NEG, 0.0)
    indk = np.zeros((NBLK, S), np.float32)
    for n in range(NBLK):
        indk[n, n * BLK:(n + 1) * BLK] = 1.0
    summ = np.zeros((64, 4, 80), np.float32)
    for own in range(4, NBLK):
        for n in range(own):
            for m in range(own):
                summ[n * NBLK + m, own - 4, 64 + n] = 1.0
                summ[n * NBLK + m, own - 4, 72 + n] = 1.0
    pm = np.zeros((128, 128), np.float32)
    for p in range(128):
        pm[p + 32 if (p % 64) < 32 else p - 32, p] = 1.0
    return {"c_band": band, "c_ident": np.eye(128, dtype=np.float32), "c_rope": rope, "c_perm": pm,
            "c_cb": cb.reshape(128, 512), "c_indk": indk, "c_summ": summ.reshape(64, 320)}


def make_in_maps(x, pool_w, pool_scale, mlp_w1, mlp_w2, ln_mix_g, ln_mix_b, ln_ffn_g, ln_ffn_b, w_kv, w_q, w_o):
    c = _consts()
    lnp = np.ascontiguousarray(np.stack([ln_mix_g[0], ln_mix_b[0], ln_ffn_g[0], ln_ffn_b[0],
                                         ln_mix_g[1], ln_mix_b[1], ln_ffn_g[1], ln_ffn_b[1]], 0), dtype=np.float32)
    shared = {
        "pool_w": np.ascontiguousarray(pool_w[0], dtype=np.float32),
        "pool_scale": np.ascontiguousarray(pool_scale[0:1], dtype=np.float32),
        "lnp": lnp,
        "mlp_w1": np.ascontiguousarray(mlp_w1, dtype=np.float32),
        "mlp_w2": np.ascontiguousarray(mlp_w2, dtype=np.float32),
        "w_o": np.ascontiguousarray(w_o[0], dtype=np.float32),
    }
    wq = np.asarray(w_q[0], dtype=np.float32)
    wk = np.asarray(w_kv[:, :D], dtype=np.float32)
    wv = np.asarray(w_kv[:, D:], dtype=np.float32)
    mats = np.stack([wq, wk, wv], 0)
    wattn = mats.reshape(3, NC_, 128, NC_, 128).transpose(3, 2, 0, 1, 4)
    shared["wattn"] = np.ascontiguousarray(wattn).reshape(NC_, 128, 3 * NC_ * 128)
    shared.update(c)
    maps = []
    for i in range(x.shape[0]):
        m = dict(shared)
        m["x"] = np.ascontiguousarray(x[i], dtype=np.float32)
        maps.append(m)
    return maps


def kernel(**inputs):
    inputs = {k: np.asarray(v) for k, v in inputs.items()}
    nc = build_program()
    in_maps = make_in_maps(**inputs)
    res = run_bass_kernel_spmd(nc, in_maps, core_ids=list(range(N_CORES)))
    return np.stack([r["out"] for r in res.results], 0).astype(np.float32)
```

```python
import numpy as np
from contextlib import ExitStack
import concourse.bass as bass
import concourse.mybir as mybir
from concourse.bass_utils import run_bass_kernel_spmd

F32 = mybir.dt.float32
BF16 = mybir.dt.bfloat16
ALU = mybir.AluOpType
AF = mybir.ActivationFunctionType
AX = mybir.AxisListType

S = 2048
D = 1024
DFF = 4096
NT = S // 128
NC_ = D // 128
H = 16
DH = 64
BLK = 256
NBLK = S // BLK
ALPHA = 4.0 ** 0.25
EPS = 1e-5
NEG = -30000.0
N_CORES = 8


class _Op:
    __slots__ = ("eng", "fn", "deps", "inc", "val", "dma")

    def __init__(self, eng, fn, dma):
        self.eng = eng
        self.fn = fn
        self.deps = []
        self.inc = False
        self.val = None
        self.dma = dma


class Prog:
    ENGS = ("sp", "act", "pool", "dve", "pe")

    def __init__(self, nc, es, n_dma_sems=80):
        self.nc = nc
        self.h = {"sp": nc.sync, "act": nc.scalar, "pool": nc.gpsimd, "dve": nc.vector, "pe": nc.tensor}
        self.sem = {e: es.enter_context(nc.semaphore("c_" + e)) for e in self.ENGS if e != "sp"}
        self.cnt = {e: 0 for e in self.sem}
        self.free_dma = [es.enter_context(nc.semaphore("d%d" % i)) for i in range(n_dma_sems)]
        self.dma_sem = {}
        self.dma_cnt = {}
        self.ops = {e: [] for e in self.ENGS}
        self.lw = {}
        self.rd = {}
        self.barrier = {}
        self.n_ops = 0

    def op(self, eng, fn, reads=(), writes=(), dma=None):
        o = _Op(eng, fn, dma)
        if dma is not None and dma not in self.dma_sem:
            self.dma_sem[dma] = self.free_dma.pop()
            self.dma_cnt[dma] = 0
        deps = []
        for k in reads:
            w = self.lw.get(k)
            if w is not None:
                deps.append(w)
        for k in writes:
            w = self.lw.get(k)
            if w is not None:
                deps.append(w)
            deps.extend(self.rd.get(k, ()))
        for d in deps:
            if d is o:
                continue
            if d.dma is None and d.eng == eng and eng == "pe":
                continue
            o.deps.append(d)
            d.inc = True
        for k in writes:
            self.lw[k] = o
            self.rd[k] = []
        for k in reads:
            self.rd.setdefault(k, []).append(o)
        self.ops[eng].append(o)
        self.n_ops += 1
        return o

    def flush(self, final=False):
        nc = self.nc
        for e in self.ENGS:
            ops = self.ops[e]
            last_compute = None
            for o in ops:
                if o.dma is None:
                    last_compute = o
            if last_compute is not None:
                last_compute.inc = True
            for o in ops:
                if o.dma is not None:
                    self.dma_cnt[o.dma] += 16
                    o.val = self.dma_cnt[o.dma]
                elif o.inc:
                    self.cnt[e] += 1
                    o.val = self.cnt[e]
        barrier = dict(self.barrier)
        ops_all = self.ops
        sems = self.sem
        dma_sem = self.dma_sem

        def emit(e, eng):
            waited = {}

            def wait(sem, val):
                if waited.get(id(sem), 0) >= val:
                    return
                waited[id(sem)] = val
                eng.wait_ge(sem, val)

            if ops_all[e] or final:
                for sem, val in barrier.values():
                    if val > 0:
                        wait(sem, val)
            for o in ops_all[e]:
                need = {}
                for d in o.deps:
                    sem = dma_sem[d.dma] if d.dma is not None else sems[d.eng]
                    if need.get(id(sem), (None, 0))[1] < d.val:
                        need[id(sem)] = (sem, d.val)
                for sem, val in need.values():
                    wait(sem, val)
                ins = o.fn(eng)
                if o.dma is not None:
                    ins.then_inc(dma_sem[o.dma], 16)
                elif o.inc:
                    ins.then_inc(sems[e], 1)

        with nc.Block() as block:
            @block.sync
            def _(eng):
                emit("sp", eng)

            @block.scalar
            def _(eng):
                emit("act", eng)

            @block.gpsimd
            def _(eng):
                emit("pool", eng)

            @block.vector
            def _(eng):
                emit("dve", eng)

            @block.tensor
            def _(eng):
                emit("pe", eng)

        self.barrier = {}
        for e, sem in self.sem.items():
            self.barrier[id(sem)] = (sem, self.cnt[e])
        for k, sem in self.dma_sem.items():
            self.barrier[id(sem)] = (sem, self.dma_cnt[k])
        self.ops = {e: [] for e in self.ENGS}
        self.lw = {}
        self.rd = {}


def build_program(stop_after=None, dbg=None):
    nc = bass.Bass("TRN2", target_bir_lowering=False)
    dt = nc.dram_tensor
    x_d = dt("x", [S, D], F32, kind="ExternalInput").ap()
    poolw_d = dt("pool_w", [4, 256, 256], F32, kind="ExternalInput").ap()
    pscale_d = dt("pool_scale", [1, D], F32, kind="ExternalInput").ap()
    lnp_d = dt("lnp", [8, D], F32, kind="ExternalInput").ap()
    w1_d = dt("mlp_w1", [2, D, DFF], F32, kind="ExternalInput").ap()
    w2_d = dt("mlp_w2", [2, DFF, D], F32, kind="ExternalInput").ap()
    band_d = dt("c_band", [128, 1536], F32, kind="ExternalInput").ap()
    ident_d = dt("c_ident", [128, 128], F32, kind="ExternalInput").ap()
    wattn_d = dt("wattn", [8, 128, 3 * NC_ * 128], F32, kind="ExternalInput").ap()
    perm_d = dt("c_perm", [128, 128], F32, kind="ExternalInput").ap()
    wo_d = dt("w_o", [D, D], F32, kind="ExternalInput").ap()
    rope_d = dt("c_rope", [128, 2, S], F32, kind="ExternalInput").ap()
    cb_d = dt("c_cb", [128, 512], F32, kind="ExternalInput").ap()
    indk_d = dt("c_indk", [NBLK, S], F32, kind="ExternalInput").ap()
    summ_d = dt("c_summ", [64, 4 * 80], F32, kind="ExternalInput").ap()
    out_d = dt("out", [S, D], F32, kind="ExternalOutput").ap()

    with ExitStack() as es:
        P = Prog(nc, es)
        sb = lambda name, shape, dtype: es.enter_context(nc.sbuf_tensor(name, shape, dtype))
        xres = sb("xres", [128, NT, D], F32)
        xT = sb("xT", [128, NC_, S], BF16)
        lng = sb("lng", [128, D], F32)
        lnb = sb("lnb", [128, D], F32)
        ident = sb("ident", [128, 128], BF16)
        stats = sb("stats", [128, 4, 12], F32)
        mv = sb("mv", [128, 4, 2], F32)
        rstd = sb("rstd", [128, 4, 1], F32)
        ctx = {}

        def load_ln_params(idx):
            P.op("sp", lambda e: e.dma_start(out=lng[:], in_=lnp_d[2 * idx:2 * idx + 1, :].partition_broadcast(128)),
                 writes=["lng"], dma="lng")
            P.op("sp", lambda e: e.dma_start(out=lnb[:], in_=lnp_d[2 * idx + 1:2 * idx + 2, :].partition_broadcast(128)),
                 writes=["lnb"], dma="lnb")

        def xk(T):
            return [("xres", T, 0), ("xres", T, 1)]

        def ln_s1(T):
            j = T % 4
            xr = xres[:, T, :]
            P.op("dve", lambda e: e.bn_stats(stats[:, j, 0:6], xr[:, 0:512]),
                 reads=[("xres", T, 0)], writes=[("stats", j, 0)])
            P.op("dve", lambda e: e.bn_stats(stats[:, j, 6:12], xr[:, 512:1024]),
                 reads=[("xres", T, 1)], writes=[("stats", j, 1)])
            P.op("dve", lambda e: e.bn_aggr(mv[:, j, :], stats[:, j, :]),
                 reads=[("stats", j, 0), ("stats", j, 1)], writes=[("mv", j)])
            P.op("act", lambda e: e.activation(out=rstd[:, j, :], in_=mv[:, j, 1:2], func=AF.Sqrt, bias=EPS, scale=1.0),
                 reads=[("mv", j)], writes=[("rstd", j)])
            P.op("dve", lambda e: e.reciprocal(out=rstd[:, j, :], in_=rstd[:, j, :]),
                 reads=[("rstd", j)], writes=[("rstd", j)])
            P.op("dve", lambda e: e.tensor_scalar(out=xr, in0=xr, scalar1=mv[:, j, 0:1], scalar2=rstd[:, j, :],
                                                  op0=ALU.subtract, op1=ALU.mult),
                 reads=xk(T) + [("mv", j), ("rstd", j)], writes=xk(T))

        def ln_s2(T, final):
            j = T % 4
            jb = T % ctx.get("ring", 4)
            xr = xres[:, T, :]
            P.op("pool", lambda e: e.tensor_tensor(out=xr, in0=xr, in1=lng[:], op=ALU.mult),
                 reads=xk(T) + ["lng"], writes=xk(T))
            P.op("pool", lambda e: e.tensor_tensor(out=xr, in0=xr, in1=lnb[:], op=ALU.add),
                 reads=xk(T) + ["lnb"], writes=xk(T))
            if final:
                P.op("sp", lambda e: e.dma_start(out=out_d[T * 128:(T + 1) * 128, :], in_=xr),
                     reads=xk(T), dma=("out", T % 4))
                return
            xb16 = ctx["xb16"]
            P.op("act", lambda e: e.activation(out=xb16[:, jb, :], in_=xr, func=AF.Copy),
                 reads=xk(T), writes=[("xb16", jb)])

        def pipeline(n, stages, oldest_first=False):
            maxs = max(sk for sk, _ in stages)
            for step in range(n + maxs):
                for sk, fn in (sorted(stages, key=lambda st: -st[0]) if oldest_first else stages):
                    T = step - sk
                    if 0 <= T < n:
                        fn(T)

        def ln_b(T, tp):
            jb = T % ctx.get("ring", 4)
            xb16 = ctx["xb16"]
            for c in range(NC_):
                P.op("pe", lambda e, c=c: e.transpose(tp[:, c, :], xb16[:, jb, c * 128:(c + 1) * 128], ident[:]),
                     reads=[("xb16", jb), "ident"], writes=["tp"])
            P.op("act", lambda e: e.activation(out=xT[:, :, T * 128:(T + 1) * 128], in_=tp[:, :, :], func=AF.Copy),
                 reads=["tp"], writes=[("xT", T)])

        def mlp_phase(l, ln_idx, final):
            with ExitStack() as ps:
                sbp = lambda name, shape, dtype: ps.enter_context(nc.sbuf_tensor(name + "_m%d" % l, shape, dtype))
                pp = lambda name, shape, dtype: ps.enter_context(nc.psum_tensor(name + "_m%d" % l, shape, dtype))
                wq1 = sbp("wq1", [128, 2, NC_, 1024], BF16)
                wq2 = sbp("wq2", [128, 2, 8, D], BF16)
                hT = sbp("hT", [128, 2, 8, 512], BF16)
                rl = sbp("rl", [128, 3, 512], F32)
                ps_h = pp("ps_h", [128, 3, 512], F32)
                ps_o = pp("ps_o", [128, 2, 512], F32)
                tp = pp("tp", [128, NC_, 128], BF16)
                ctx["xb16"] = sbp("xb16", [128, 8, D], BF16)
                ctx["ring"] = 8
                w1v = w1_d[l].rearrange("(dc p) f -> p dc f", p=128)

                def load_q(q):
                    slot = q % 2
                    for g in range(4):
                        P.op("pool", lambda e, g=g: e.dma_start(
                            out=wq1[:, slot, :, g * 256:(g + 1) * 256],
                            in_=w1v[:, :, q * 1024 + g * 256:q * 1024 + (g + 1) * 256]),
                            writes=[("wq1", slot, g)], dma=("wq1", slot, g))
                    for g in range(4):
                        P.op("pool", lambda e, g=g: e.dma_start(
                            out=wq2[:, slot, 2 * g:2 * g + 2, :],
                            in_=w2_d[l, q * 1024 + g * 256:q * 1024 + (g + 1) * 256, :].rearrange(
                                "(fc p) d -> p fc d", p=128)),
                            writes=[("wq2", slot, g)], dma=("wq2", slot, g))

                load_ln_params(ln_idx)
                load_q(0)
                pending_b = []
                pending_s2 = []
                for q in range(4):
                    slot = q % 2
                    if q + 1 < 4:
                        load_q(q + 1)
                    for tg in range(4):
                        hb = (q * 4 + tg) % 2
                        xT_keys = [("xT", tg * 4 + i) for i in range(4)]
                        for fc in range(8):
                            bank = fc % 3
                            pbank = (fc - 1) % 3
                            for dc in range(NC_):
                                P.op("pe", lambda e, fc=fc, dc=dc, bank=bank, slot=slot, tg=tg: e.matmul(
                                    ps_h[:, bank, :], wq1[:, slot, dc, fc * 128:(fc + 1) * 128],
                                    xT[:, dc, tg * 512:(tg + 1) * 512], start=(dc == 0), stop=(dc == NC_ - 1)),
                                    reads=[("wq1", slot, fc // 2)] + xT_keys, writes=[("ps_h", bank)])
                            P.op("act", lambda e, bank=bank: e.activation(out=rl[:, bank, :], in_=ps_h[:, bank, :],
                                                                           func=AF.Relu),
                                 reads=[("ps_h", bank)], writes=[("rl", bank)])
                            if fc > 0:
                                P.op("act", lambda e, fc=fc, pbank=pbank, hb=hb: e.activation(
                                    out=hT[:, hb, fc - 1, :], in_=rl[:, pbank, :], func=AF.Square),
                                    reads=[("rl", pbank)], writes=[("hT", hb, fc - 1)])
                        P.op("act", lambda e, hb=hb: e.activation(out=hT[:, hb, 7, :], in_=rl[:, 1, :], func=AF.Square),
                             reads=[("rl", 1)], writes=[("hT", hb, 7)])
                        for tt in range(4):
                            T = tg * 4 + tt
                            for half in range(2):
                                ob = (tt * 2 + half) % 2
                                for fc in range(8):
                                    P.op("pe", lambda e, fc=fc, tt=tt, half=half, ob=ob, hb=hb, slot=slot: e.matmul(
                                        ps_o[:, ob, :], hT[:, hb, fc, tt * 128:(tt + 1) * 128],
                                        wq2[:, slot, fc, half * 512:(half + 1) * 512], start=(fc == 0), stop=(fc == 7)),
                                        reads=[("hT", hb, fc), ("wq2", slot, fc // 2)], writes=[("ps_o", ob)])
                                xs = xres[:, T, half * 512:(half + 1) * 512]
                                if q == 0:
                                    P.op("dve", lambda e, xs=xs, ob=ob: e.scalar_tensor_tensor(
                                        out=xs, in0=xs, scalar=ALPHA, in1=ps_o[:, ob, :], op0=ALU.mult, op1=ALU.add),
                                        reads=[("xres", T, half), ("ps_o", ob)], writes=[("xres", T, half)])
                                else:
                                    P.op("dve", lambda e, xs=xs, ob=ob: e.tensor_tensor(
                                        out=xs, in0=xs, in1=ps_o[:, ob, :], op=ALU.add),
                                        reads=[("xres", T, half), ("ps_o", ob)], writes=[("xres", T, half)])
                            if q == 3:
                                ln_s1(T)
                                for T2 in pending_s2:
                                    ln_s2(T2, final)
                                    if not final:
                                        pending_b.append(T2)
                                pending_s2 = [T]
                                if pending_b and pending_b[0] < tg * 4:
                                    ln_b(pending_b.pop(0), tp)
                for T in pending_b:
                    ln_b(T, tp)
                for T2 in pending_s2:
                    ln_s2(T2, final)
                    if not final:
                        ln_b(T2, tp)
                if stop_after is not None and stop_after == ln_idx:
                    return True
                P.flush()
            return False

        def attn_phase():
            with ExitStack() as po:
                OT = po.enter_context(nc.sbuf_tensor("OT", [128, NC_, S], BF16))
                with ExitStack() as ps:
                    sbp = lambda name, shape, dtype: ps.enter_context(nc.sbuf_tensor(name, shape, dtype))
                    pp = lambda name, shape, dtype: ps.enter_context(nc.psum_tensor(name, shape, dtype))
                    wpair = sbp("wpair", [128, 3, NC_, 128], BF16)
                    permm = sbp("permm", [128, 128], BF16)
                    qb = sbp("qb", [128, 2, 512], BF16)
                    Qz = sbp("Qz", [128, 2, S], BF16)
                    Kz = sbp("Kz", [128, 2, S], BF16)
                    Vp = sbp("Vp", [128, NT, 2, 128], BF16)
                    rope = sbp("rope", [128, 2, 2, 512], F32)
                    t1 = sbp("t1", [128, 2, 512], F32)
                    t2 = sbp("t2", [128, 2, 512], F32)
                    PT = sbp("PT", [128, 6, 512], BF16)
                    kms = sbp("kms", [128, NBLK], F32)
                    KMd = sbp("KMd", [128, 128], BF16)
                    G = sbp("G", [64, 512], BF16)
                    CB = sbp("CB", [128, 2, 256], BF16)
                    SUMM = sbp("SUMM", [64, 4, 80], BF16)
                    den = sbp("den", [128, 2, 512], F32)
                    NWB = 5
                    ps_proj = pp("ps_proj", [128, NWB, 512], F32)
                    ps_s = ps_proj
                    ps_o = pp("ps_o2", [128, 2, 512], F32)
                    ps_sel = pp("ps_sel", [128, 512], F32)

                    P.op("pool", lambda e: e.dma_start(out=CB[:].rearrange("p a b -> p (a b)"), in_=cb_d[:, :]),
                         writes=["CB"], dma="CB")
                    P.op("pool", lambda e: e.memset(Qz[:].rearrange("p a b -> p (a b)"), 0.0), writes=["Qz0"])
                    P.op("pool", lambda e: e.memset(Kz[:].rearrange("p a b -> p (a b)"), 0.0), writes=["Kz0"])
                    P.op("pool", lambda e: e.dma_start(out=Kz[64:72, 0, :], in_=indk_d[:, :]),
                         reads=["Kz0"], writes=["Kzi0"], dma="Kzi0")
                    P.op("pool", lambda e: e.dma_start(out=Kz[0:8, 1, :], in_=indk_d[:, :]),
                         reads=["Kz0"], writes=["Kzi1"], dma="Kzi1")
                    P.op("pool", lambda e: e.dma_start(out=SUMM[:].rearrange("p a b -> p (a b)"), in_=summ_d[:, :]),
                         writes=["SUMM"], dma="SUMM")
                    P.op("pool", lambda e: e.dma_start(out=permm[:], in_=perm_d[:, :]), writes=["permm"], dma="permm")
                    P.op("pool", lambda e: e.memset(Vp[:, :, 0, 64:128], 1.0), writes=["Vp1a"])
                    P.op("pool", lambda e: e.memset(Vp[:, :, 1, 0:64], 1.0), writes=["Vp1b"])
                    P.op("pool", lambda e: e.memset(KMd[:], 0.0), writes=["KMd0"])

                    cnt = {"pj": 0, "t": 0, "s": 0, "o": 0, "rope": 0}

                    def next_bank():
                        b = cnt["pj"] % NWB
                        cnt["pj"] += 1
                        return b

                    def qk_item(c, tg):
                        rs = cnt["rope"] % 2
                        cnt["rope"] += 1
                        xT_keys = [("xT", tg * 4 + i) for i in range(4)]
                        P.op("sp", lambda e: e.dma_start(
                            out=rope[:, rs, :, :], in_=rope_d[:, :, tg * 512:(tg + 1) * 512]),
                            writes=[("rope", rs)], dma=("rope", rs))
                        banks = []
                        for which in (0, 1):
                            ba = next_bank()
                            banks.append(ba)
                            for dc in range(NC_):
                                P.op("pe", lambda e, which=which, ba=ba, dc=dc: e.matmul(
                                    ps_proj[:, ba, :], wpair[:, which, dc, :], xT[:, dc, tg * 512:(tg + 1) * 512],
                                    start=(dc == 0), stop=(dc == NC_ - 1)),
                                    reads=["wpair"] + xT_keys, writes=[("pp", ba)])
                            P.op("act", lambda e, which=which, ba=ba: e.activation(
                                out=qb[:, which, :], in_=ps_proj[:, ba, :], func=AF.Copy),
                                reads=[("pp", ba)], writes=[("qb", which)])
                        for which in (0, 1):
                            ba = banks[which]
                            bb = next_bank()
                            P.op("pe", lambda e, which=which, bb=bb: e.matmul(
                                ps_proj[:, bb, :], permm[:], qb[:, which, :], start=True, stop=True),
                                reads=["permm", ("qb", which)], writes=[("pp", bb)])
                            ts = cnt["t"] % 2
                            cnt["t"] += 1
                            P.op("dve", lambda e, ts=ts, ba=ba: e.tensor_tensor(
                                out=t1[:, ts, :], in0=ps_proj[:, ba, :], in1=rope[:, rs, 0, :], op=ALU.mult),
                                reads=[("pp", ba), ("rope", rs)], writes=[("t1", ts)])
                            P.op("dve", lambda e, ts=ts, bb=bb: e.tensor_tensor(
                                out=t2[:, ts, :], in0=ps_proj[:, bb, :], in1=rope[:, rs, 1, :], op=ALU.mult),
                                reads=[("pp", bb), ("rope", rs)], writes=[("t2", ts)])
                            dstz, dkey, zkey = (Qz, "Qp", "Qz0") if which == 0 else (Kz, "Kp", "Kz0")
                            for hh in (0, 1):
                                rr = slice(hh * 64, hh * 64 + 64)
                                P.op("dve", lambda e, ts=ts, hh=hh, rr=rr, dstz=dstz: e.tensor_tensor(
                                    out=dstz[rr, hh, tg * 512:(tg + 1) * 512], in0=t1[rr, ts, :], in1=t2[rr, ts, :],
                                    op=ALU.add),
                                    reads=[("t1", ts), ("t2", ts), zkey], writes=[(dkey, hh, tg)])
                                if which == 1:
                                    P.op("dve", lambda e, hh=hh, rr=rr: e.tensor_reduce(
                                        out=kms[rr, 2 * tg:2 * tg + 2],
                                        in_=Kz[rr, hh, tg * 512:(tg + 1) * 512].rearrange("p (b k) -> p b k", b=2),
                                        axis=AX.X, op=ALU.add),
                                        reads=[("Kp", hh, tg)], writes=[("kms", hh, tg)])

                    def v_item(c, g4):
                        bv = next_bank()
                        for i in range(4):
                            T = g4 * 4 + i
                            for dc in range(NC_):
                                P.op("pe", lambda e, i=i, T=T, dc=dc: e.matmul(
                                    ps_proj[:, bv, i * 128:(i + 1) * 128], xT[:, dc, T * 128:(T + 1) * 128],
                                    wpair[:, 2, dc, :], start=(dc == 0), stop=(dc == NC_ - 1)),
                                    reads=["wpair", ("xT", T)], writes=[("pp", bv)])
                        pv = ps_proj[:, bv, :].rearrange("p (a b) -> p a b", a=4)
                        P.op("dve", lambda e: e.tensor_copy(
                            out=Vp[:, g4 * 4:(g4 + 1) * 4, 0, 0:64], in_=pv[:, :, 0:64]),
                            reads=[("pp", bv)], writes=[("Vp", g4, 0)])
                        P.op("dve", lambda e: e.tensor_copy(
                            out=Vp[:, g4 * 4:(g4 + 1) * 4, 1, 64:128], in_=pv[:, :, 64:128]),
                            reads=[("pp", bv)], writes=[("Vp", g4, 1)])

                    def kmd_item(c):
                        for hh in (0, 1):
                            r0 = hh * 64
                            P.op("dve", lambda e, r0=r0: e.tensor_tensor(
                                out=KMd[r0:r0 + 64, r0:r0 + 64].rearrange("p (n m) -> p n m", n=NBLK),
                                in0=kms[r0:r0 + 64, :].unsqueeze(1).broadcast_to([64, NBLK, NBLK]),
                                in1=kms[r0:r0 + 64, :].unsqueeze(2).broadcast_to([64, NBLK, NBLK]),
                                op=ALU.subtract),
                                reads=[("kms", hh, i) for i in range(4)] + ["KMd0"], writes=[("KMd", hh)])

                    def sel_a(c, own):
                        jq = own // 2
                        for hh in (0, 1):
                            P.op("pe", lambda e, hh=hh: e.matmul(
                                ps_sel[0:64, hh * 256:(hh + 1) * 256], KMd[:, hh * 64:(hh + 1) * 64],
                                Qz[:, hh, own * 256:(own + 1) * 256], start=True, stop=True),
                                reads=[("KMd", hh), "KMd0", ("Qp", hh, jq), "Qz0", ("Qzb", hh, own)], writes=["selbank"])
                        P.op("dve", lambda e: e.tensor_single_scalar(out=G[:], in_=ps_sel[0:64, :], scalar=0.0,
                                                                     op=ALU.is_gt),
                             reads=["selbank"], writes=["G"])

                    def sel_b(c, own):
                        P.op("pe", lambda e: e.matmul(
                            ps_sel[0:72, 0:256], SUMM[:, own - 4, 0:72], G[:, 0:256], start=True, stop=True),
                            reads=["G", "SUMM", "selbank"], writes=["selbank"])
                        P.op("pe", lambda e: e.matmul(
                            ps_sel[0:8, 256:512], SUMM[:, own - 4, 72:80], G[:, 256:512], start=True, stop=True),
                            reads=["G", "SUMM", "selbank"], writes=["selbank"])
                        P.op("dve", lambda e: e.tensor_scalar(
                            out=Qz[64:72, 0, own * 256:(own + 1) * 256], in0=ps_sel[64:72, 0:256], scalar1=2.5,
                            scalar2=NEG, op0=ALU.is_ge, op1=ALU.mult),
                            reads=["selbank", "Qz0"], writes=[("Qzb", 0, own)])
                        P.op("dve", lambda e: e.tensor_scalar(
                            out=Qz[0:8, 1, own * 256:(own + 1) * 256], in0=ps_sel[0:8, 256:512], scalar1=2.5,
                            scalar2=NEG, op0=ALU.is_ge, op1=ALU.mult),
                            reads=["selbank", "Qz0"], writes=[("Qzb", 1, own)])

                    def emit_s(c, it):
                        hh, j, kt, kind = it
                        wbk = next_bank()
                        pslot = cnt["s"] % 6
                        cnt["s"] += 1
                        kap = Kz[:, hh, kt * 128:(kt + 1) * 128]
                        kk = [("Kp", hh, kt // 4), "Kz0", "Kzi0", "Kzi1", ("Qp", hh, j), "Qz0"]
                        kk += [("Qzb", hh, o) for o in (2 * j, 2 * j + 1) if o >= 4]
                        if kind == "diagB":
                            cols = slice(256, 512)
                            P.op("pe", lambda e: e.matmul(
                                ps_s[:, wbk, 256:512], kap, Qz[:, hh, j * 512 + 256:(j + 1) * 512],
                                start=True, stop=False), reads=kk, writes=[("pp", wbk)])
                            P.op("pe", lambda e: e.matmul(
                                ps_s[:, wbk, 256:512], ident[:], CB[:, kt % 2, :], start=False, stop=True),
                                reads=["ident", "CB"], writes=[("pp", wbk)])
                        else:
                            cols = slice(0, 512)
                            P.op("pe", lambda e: e.matmul(
                                ps_s[:, wbk, :], kap, Qz[:, hh, j * 512:(j + 1) * 512], start=True,
                                stop=(kind == "full")), reads=kk, writes=[("pp", wbk)])
                            if kind == "mixed":
                                P.op("pe", lambda e: e.matmul(
                                    ps_s[:, wbk, 0:256], ident[:], CB[:, kt % 2, :], start=False, stop=True),
                                    reads=["ident", "CB"], writes=[("pp", wbk)])
                        P.op("act", lambda e: e.activation(out=PT[:, pslot, cols], in_=ps_s[:, wbk, cols],
                                                           func=AF.Exp, scale=DH ** -0.5),
                             reads=[("pp", wbk)], writes=[("PT", pslot)])
                        return pslot

                    def emit_pv(c, it, pslot, ob):
                        hh, j, kt, kind = it
                        cols = slice(256, 512) if kind == "diagB" else slice(0, 512)
                        first = (kt == 0)
                        last = (kt == 4 * j + 3)
                        P.op("pe", lambda e: e.matmul(
                            ps_o[:, ob, cols], Vp[:, kt, hh, :], PT[:, pslot, cols], start=first, stop=last),
                            reads=[("Vp", kt // 4, hh), "Vp1a", "Vp1b", ("PT", pslot)], writes=[("ps_o", ob)])
                        if last:
                            orow = slice(0, 64) if hh == 0 else slice(64, 128)
                            drow = slice(64, 128) if hh == 0 else slice(0, 64)
                            P.op("dve", lambda e: e.reciprocal(out=den[drow, ob, :], in_=ps_o[drow, ob, :]),
                                 reads=[("ps_o", ob)], writes=[("den", ob)])
                            P.op("dve", lambda e: e.tensor_tensor(
                                out=OT[orow, c, j * 512:(j + 1) * 512], in0=ps_o[orow, ob, :],
                                in1=den[drow, ob, :], op=ALU.mult),
                                reads=[("ps_o", ob), ("den", ob)], writes=[("OT", c, hh, j)])

                    DEPTH = 4
                    for c in range(NC_):
                        P.op("pool", lambda e, c=c: e.dma_start(
                            out=wpair[:].rearrange("p a b c -> p (a b c)"), in_=wattn_d[c]),
                            writes=["wpair"], dma="wpair")
                        for tg in range(4):
                            qk_item(c, tg)
                        for g4 in range(4):
                            v_item(c, g4)
                        items = []
                        for hh in (0, 1):
                            for j in range(4):
                                items += [(hh, j, kt, "full") for kt in range(4 * j)]
                                items += [(hh, j, kt, "mixed") for kt in (4 * j, 4 * j + 1)]
                                items += [(hh, j, kt, "diagB") for kt in (4 * j + 2, 4 * j + 3)]
                        hooks = {
                            0: [lambda c=c: kmd_item(c), lambda c=c: sel_a(c, 4)],
                            2: [lambda c=c: sel_b(c, 4), lambda c=c: sel_a(c, 5)],
                            4: [lambda c=c: sel_b(c, 5), lambda c=c: sel_a(c, 6)],
                            6: [lambda c=c: sel_b(c, 6), lambda c=c: sel_a(c, 7)],
                            8: [lambda c=c: sel_b(c, 7)],
                        }
                        gid = {}
                        for it in items:
                            if (it[0], it[1]) not in gid:
                                gid[(it[0], it[1])] = cnt["o"]
                                cnt["o"] += 1
                        pslots = {}
                        for i in range(len(items) + DEPTH):
                            for f in hooks.get(i, ()):
                                f()
                            if i < len(items):
                                pslots[i] = emit_s(c, items[i])
                            if i >= DEPTH:
                                it = items[i - DEPTH]
                                emit_pv(c, it, pslots[i - DEPTH], gid[(it[0], it[1])] % 2)
                    P.flush()
                with ExitStack() as ps:
                    sbp = lambda name, shape, dtype: ps.enter_context(nc.sbuf_tensor(name, shape, dtype))
                    pp = lambda name, shape, dtype: ps.enter_context(nc.psum_tensor(name, shape, dtype))
                    wo = sbp("wo", [128, NC_, D], BF16)
                    ctx["xb16"] = sbp("xb16_a", [128, 4, D], BF16)
                    ctx["ring"] = 4
                    ps_y = pp("ps_y2", [128, 2, 512], F32)
                    tp = pp("tp_a", [128, NC_, 128], BF16)
                    for g in range(4):
                        P.op("pool", lambda e, g=g: e.dma_start(
                            out=wo[:, 2 * g:2 * g + 2, :],
                            in_=wo_d[g * 256:(g + 1) * 256, :].rearrange("(c p) e -> p c e", p=128)),
                            writes=[("wo", g)], dma=("wo", g))
                    load_ln_params(2)

                    def wo_s0(T):
                        for half in range(2):
                            ob = (T * 2 + half) % 2
                            for c in range(NC_):
                                P.op("pe", lambda e, half=half, ob=ob, c=c: e.matmul(
                                    ps_y[:, ob, :], OT[:, c, T * 128:(T + 1) * 128],
                                    wo[:, c, half * 512:(half + 1) * 512], start=(c == 0), stop=(c == NC_ - 1)),
                                    reads=[("wo", c // 2)], writes=[("ps_y", ob)])
                            xs = xres[:, T, half * 512:(half + 1) * 512]
                            P.op("dve", lambda e, xs=xs, ob=ob: e.scalar_tensor_tensor(
                                out=xs, in0=xs, scalar=ALPHA, in1=ps_y[:, ob, :], op0=ALU.mult, op1=ALU.add),
                                reads=[("xres", T, half), ("ps_y", ob)], writes=[("xres", T, half)])

                    pipeline(NT, [(0, wo_s0), (2, ln_s1), (4, lambda T: ln_s2(T, False)), (6, lambda T: ln_b(T, tp))])
                    if stop_after == 2:
                        return True
                    P.flush()
            return False

        def dump_and_finish():
            for T in range(NT):
                P.op("sp", lambda e, T=T: e.dma_start(out=out_d[T * 128:(T + 1) * 128, :], in_=xres[:, T, :]),
                     reads=xk(T), dma=("out", T % 4))
            P.flush()
            P.flush(final=True)
            return nc

        with ExitStack() as ps:
            sbp = lambda name, shape, dtype: ps.enter_context(nc.sbuf_tensor(name, shape, dtype))
            pp = lambda name, shape, dtype: ps.enter_context(nc.psum_tensor(name, shape, dtype))
            xbf = sbp("xbf", [128, 3, D], BF16)
            pooled = sbp("pooled", [128, 2, NC_, 128], BF16)
            band = sbp("band", [128, 1536], BF16)
            pw_stage = sbp("pw_stage", [128, 4, 2, 256], F32)
            sc_tab = sbp("sc_tab", [128, D], F32)
            poolw = sbp("poolw", [128, 4, 2, 256], BF16)
            ps_pool = pp("ps_pool", [128, NC_, 128], F32)
            ps_y = pp("ps_y", [128, 2, D], F32)
            tp = pp("tp", [128, NC_, 128], BF16)
            ctx["xb16"] = sbp("xb16_p0", [128, 4, D], BF16)
            ctx["ring"] = 4

            load_ln_params(0)
            P.op("sp", lambda e: e.dma_start(out=sc_tab[:], in_=pscale_d[0:1, :].partition_broadcast(128)),
                 writes=["sc_tab"], dma="sc_tab")
            P.op("sp", lambda e: e.dma_start(out=pw_stage[:], in_=poolw_d.rearrange("g (k p) e -> p g k e", p=128)),
                 writes=["pw_stage"], dma="pw_stage")
            P.op("pool", lambda e: e.dma_start(out=band[:], in_=band_d[:, :]), writes=["band"], dma="band")
            P.op("pool", lambda e: e.dma_start(out=ident[:], in_=ident_d[:, :]), writes=["ident"], dma="ident")
            for k in range(2):
                P.op("dve", lambda e, k=k: e.tensor_tensor(
                    out=poolw[:, :, k, :], in0=pw_stage[:, :, k, :],
                    in1=sc_tab[:].rearrange("p (g e) -> p g e", g=4), op=ALU.mult),
                    reads=["pw_stage", "sc_tab"], writes=[("poolw", k)])
            for T in range(NT):
                P.op("sp", lambda e, T=T: e.dma_start(out=xres[:, T, :], in_=x_d[T * 128:(T + 1) * 128, :]),
                     writes=xk(T), dma=("xld", T))

            def load_xbf(T):
                P.op("pool", lambda e, T=T: e.dma_start(out=xbf[:, T % 3, :], in_=x_d[T * 128:(T + 1) * 128, :]),
                     writes=[("xbf", T % 3)], dma=("xbf", T % 3))

            load_xbf(0)
            load_xbf(1)

            def p0_s0(T):
                for c in range(NC_):
                    g = c // 2
                    boff = (512 if T == 0 else 0) + g * 128
                    P.op("pe", lambda e, c=c, boff=boff: e.matmul(
                        ps_pool[:, c, :], xbf[:, T % 3, c * 128:(c + 1) * 128], band[:, boff:boff + 128],
                        start=True, stop=(T == 0)),
                        reads=[("xbf", T % 3), "band"], writes=["ps_pool"])
                    if T > 0:
                        P.op("pe", lambda e, c=c, g=g: e.matmul(
                            ps_pool[:, c, :], xbf[64:128, (T - 1) % 3, c * 128:(c + 1) * 128],
                            band[64:128, 1024 + g * 128:1024 + (g + 1) * 128], start=False, stop=True),
                            reads=[("xbf", (T - 1) % 3), "band"], writes=["ps_pool"])
                if T + 2 < NT:
                    load_xbf(T + 2)
                P.op("act", lambda e: e.activation(out=pooled[:, T % 2, :, :], in_=ps_pool[:, :, :], func=AF.Copy),
                     reads=["ps_pool"], writes=[("pooled", T % 2)])
                for g in range(4):
                    for k in range(2):
                        P.op("pe", lambda e, g=g, k=k: e.matmul(
                            ps_y[:, T % 2, g * 256:(g + 1) * 256], pooled[:, T % 2, 2 * g + k, :], poolw[:, g, k, :],
                            start=(k == 0), stop=(k == 1)),
                            reads=[("pooled", T % 2), ("poolw", k)], writes=[("ps_y", T % 2)])
                P.op("dve", lambda e: e.scalar_tensor_tensor(
                    out=xres[:, T, :], in0=xres[:, T, :], scalar=ALPHA, in1=ps_y[:, T % 2, :],
                    op0=ALU.mult, op1=ALU.add),
                    reads=xk(T) + [("ps_y", T % 2)], writes=xk(T))

            if dbg == 'noln':
                pipeline(NT, [(0, p0_s0)])
            else:
                pipeline(NT, [(0, p0_s0), (2, ln_s1), (4, lambda T: ln_s2(T, False)), (6, lambda T: ln_b(T, tp))],
                         oldest_first=True)
            if stop_after == 0:
                return dump_and_finish()
            P.flush()

        mlp_phase(0, 1, False)
        if stop_after == 1:
            return dump_and_finish()
        attn_phase()
        if stop_after == 2:
            return dump_and_finish()
        mlp_phase(1, 3, True)
        P.flush(final=True)
    return nc


def _consts():
    band = np.zeros((128, 1536), np.float32)
    tl = np.arange(128)
    for g, w in enumerate((2, 4, 8, 16)):
        dd = tl[None, :] - tl[:, None]
        inwin = (dd >= 0) & (dd < w)
        Dg = inwin.astype(np.float32) / w - np.eye(128, dtype=np.float32)
        cnt = np.minimum(tl + 1, w).astype(np.float32)
        D0 = inwin.astype(np.float32) / cnt[None, :] - np.eye(128, dtype=np.float32)
        dd2 = tl[None, :] + 128 - tl[:, None]
        U = ((dd2 >= 0) & (dd2 < w)).astype(np.float32) / w
        band[:, g * 128:(g + 1) * 128] = Dg
        band[:, 512 + g * 128:512 + (g + 1) * 128] = D0
        band[:, 1024 + g * 128:1024 + (g + 1) * 128] = U
    half = DH // 2
    inv = (np.float32(10000.0) ** (-np.arange(half, dtype=np.float32) * np.float32(2.0) / np.float32(DH))).astype(np.float32)
    ang = (np.arange(S, dtype=np.float32)[None, :] * inv[:, None]).astype(np.float32)
    cos = np.cos(ang).astype(np.float32)
    sin = np.sin(ang).astype(np.float32)
    rope = np.zeros((128, 2, S), np.float32)
    for p in range(128):
        rope[p, 0] = cos[p % 32]
        rope[p, 1] = -sin[p % 32] if (p % 64) < 32 else sin[p % 32]
    kk = np.arange(128)[:, None]
    qq = np.arange(256)[None, :]
    cb = np.zeros((128, 2, 256), np.float32)
    cb[:, 0] = np.where(kk > qq, NEG, 0.0)
    cb[:, 1] = np.where(kk + 128 > qq, NEG, 0.0)
    indk = np.zeros((NBLK, S), np.float32)
    for n in range(NBLK):
        indk[n, n * BLK:(n + 1) * BLK] = 1.0
    summ = np.zeros((64, 4, 80), np.float32)
    for own in range(4, NBLK):
        for n in range(own):
            for m in range(own):
                summ[n * NBLK + m, own - 4, 64 + n] = 1.0
                summ[n * NBLK + m, own - 4, 72 + n] = 1.0
    pm = np.zeros((128, 128), np.float32)
    for p in range(128):
        pm[p + 32 if (p % 64) < 32 else p - 32, p] = 1.0
    return {"c_band": band, "c_ident": np.eye(128, dtype=np.float32), "c_rope": rope, "c_perm": pm,
            "c_cb": cb.reshape(128, 512), "c_indk": indk, "c_summ": summ.reshape(64, 320)}


def make_in_maps(x, pool_w, pool_scale, mlp_w1, mlp_w2, ln_mix_g, ln_mix_b, ln_ffn_g, ln_ffn_b, w_kv, w_q, w_o):
    c = _consts()
    lnp = np.ascontiguousarray(np.stack([ln_mix_g[0], ln_mix_b[0], ln_ffn_g[0], ln_ffn_b[0],
                                         ln_mix_g[1], ln_mix_b[1], ln_ffn_g[1], ln_ffn_b[1]], 0), dtype=np.float32)
    shared = {
        "pool_w": np.ascontiguousarray(pool_w[0], dtype=np.float32),
        "pool_scale": np.ascontiguousarray(pool_scale[0:1], dtype=np.float32),
        "lnp": lnp,
        "mlp_w1": np.ascontiguousarray(mlp_w1, dtype=np.float32),
        "mlp_w2": np.ascontiguousarray(mlp_w2, dtype=np.float32),
        "w_o": np.ascontiguousarray(w_o[0], dtype=np.float32),
    }
    wq = np.asarray(w_q[0], dtype=np.float32)
    wk = np.asarray(w_kv[:, :D], dtype=np.float32)
    wv = np.asarray(w_kv[:, D:], dtype=np.float32)
    mats = np.stack([wq, wk, wv], 0)
    wattn = mats.reshape(3, NC_, 128, NC_, 128).transpose(3, 2, 0, 1, 4)
    shared["wattn"] = np.ascontiguousarray(wattn).reshape(NC_, 128, 3 * NC_ * 128)
    shared.update(c)
    maps = []
    for i in range(x.shape[0]):
        m = dict(shared)
        m["x"] = np.ascontiguousarray(x[i], dtype=np.float32)
        maps.append(m)
    return maps


def kernel(**inputs):
    inputs = {k: np.asarray(v) for k, v in inputs.items()}
    nc = build_program()
    in_maps = make_in_maps(**inputs)
    res = run_bass_kernel_spmd(nc, in_maps, core_ids=list(range(N_CORES)))
    return np.stack([r["out"] for r in res.results], 0).astype(np.float32)
```

```python
import numpy as np
from contextlib import ExitStack
import concourse.bass as bass
import concourse.mybir as mybir
from concourse.bass_utils import run_bass_kernel_spmd

F32 = mybir.dt.float32
BF16 = mybir.dt.bfloat16
ALU = mybir.AluOpType
AF = mybir.ActivationFunctionType
AX = mybir.AxisListType

S = 2048
D = 1024
DFF = 4096
NT = S // 128
NC_ = D // 128
H = 16
DH = 64
BLK = 256
NBLK = S // BLK
ALPHA = 4.0 ** 0.25
EPS = 1e-5
NEG = -30000.0
N_CORES = 8


class _Op:
    __slots__ = ("eng", "fn", "deps", "inc", "val", "dma")

    def __init__(self, eng, fn, dma):
        self.eng = eng
        self.fn = fn
        self.deps = []
        self.inc = False
        self.val = None
        self.dma = dma


class Prog:
    ENGS = ("sp", "act", "pool", "dve", "pe")

    def __init__(self, nc, es, n_dma_sems=80):
        self.nc = nc
        self.h = {"sp": nc.sync, "act": nc.scalar, "pool": nc.gpsimd, "dve": nc.vector, "pe": nc.tensor}
        self.sem = {e: es.enter_context(nc.semaphore("c_" + e)) for e in self.ENGS if e != "sp"}
        self.cnt = {e: 0 for e in self.sem}
        self.free_dma = [es.enter_context(nc.semaphore("d%d" % i)) for i in range(n_dma_sems)]
        self.dma_sem = {}
        self.dma_cnt = {}
        self.ops = {e: [] for e in self.ENGS}
        self.lw = {}
        self.rd = {}
        self.barrier = {}
        self.n_ops = 0

    def op(self, eng, fn, reads=(), writes=(), dma=None):
        o = _Op(eng, fn, dma)
        if dma is not None and dma not in self.dma_sem:
            self.dma_sem[dma] = self.free_dma.pop()
            self.dma_cnt[dma] = 0
        deps = []
        for k in reads:
            w = self.lw.get(k)
            if w is not None:
                deps.append(w)
        for k in writes:
            w = self.lw.get(k)
            if w is not None:
                deps.append(w)
            deps.extend(self.rd.get(k, ()))
        for d in deps:
            if d is o:
                continue
            if d.dma is None and d.eng == eng and eng == "pe":
                continue
            o.deps.append(d)
            d.inc = True
        for k in writes:
            self.lw[k] = o
            self.rd[k] = []
        for k in reads:
            self.rd.setdefault(k, []).append(o)
        self.ops[eng].append(o)
        self.n_ops += 1
        return o

    def flush(self, final=False):
        nc = self.nc
        for e in self.ENGS:
            ops = self.ops[e]
            last_compute = None
            for o in ops:
                if o.dma is None:
                    last_compute = o
            if last_compute is not None:
                last_compute.inc = True
            for o in ops:
                if o.dma is not None:
                    self.dma_cnt[o.dma] += 16
                    o.val = self.dma_cnt[o.dma]
                elif o.inc:
                    self.cnt[e] += 1
                    o.val = self.cnt[e]
        barrier = dict(self.barrier)
        ops_all = self.ops
        sems = self.sem
        dma_sem = self.dma_sem

        def emit(e, eng):
            waited = {}

            def wait(sem, val):
                if waited.get(id(sem), 0) >= val:
                    return
                waited[id(sem)] = val
                eng.wait_ge(sem, val)

            if ops_all[e] or final:
                for sem, val in barrier.values():
                    if val > 0:
                        wait(sem, val)
            for o in ops_all[e]:
                need = {}
                for d in o.deps:
                    sem = dma_sem[d.dma] if d.dma is not None else sems[d.eng]
                    if need.get(id(sem), (None, 0))[1] < d.val:
                        need[id(sem)] = (sem, d.val)
                for sem, val in need.values():
                    wait(sem, val)
                ins = o.fn(eng)
                if o.dma is not None:
                    ins.then_inc(dma_sem[o.dma], 16)
                elif o.inc:
                    ins.then_inc(sems[e], 1)

        with nc.Block() as block:
            @block.sync
            def _(eng):
                emit("sp", eng)

            @block.scalar
            def _(eng):
                emit("act", eng)

            @block.gpsimd
            def _(eng):
                emit("pool", eng)

            @block.vector
            def _(eng):
                emit("dve", eng)

            @block.tensor
            def _(eng):
                emit("pe", eng)

        self.barrier = {}
        for e, sem in self.sem.items():
            self.barrier[id(sem)] = (sem, self.cnt[e])
        for k, sem in self.dma_sem.items():
            self.barrier[id(sem)] = (sem, self.dma_cnt[k])
        self.ops = {e: [] for e in self.ENGS}
        self.lw = {}
        self.rd = {}


def build_program(stop_after=None, dbg=None):
    nc = bass.Bass("TRN2", target_bir_lowering=False)
    dt = nc.dram_tensor
    x_d = dt("x", [S, D], F32, kind="ExternalInput").ap()
    poolw_d = dt("pool_w", [4, 256, 256], F32, kind="ExternalInput").ap()
    pscale_d = dt("pool_scale", [1, D], F32, kind="ExternalInput").ap()
    lnp_d = dt("lnp", [8, D], F32, kind="ExternalInput").ap()
    w1_d = dt("mlp_w1", [2, D, DFF], F32, kind="ExternalInput").ap()
    w2_d = dt("mlp_w2", [2, DFF, D], F32, kind="ExternalInput").ap()
    band_d = dt("c_band", [128, 1536], F32, kind="ExternalInput").ap()
    ident_d = dt("c_ident", [128, 128], F32, kind="ExternalInput").ap()
    wattn_d = dt("wattn", [8, 128, 3 * NC_ * 128], F32, kind="ExternalInput").ap()
    perm_d = dt("c_perm", [128, 128], F32, kind="ExternalInput").ap()
    wo_d = dt("w_o", [D, D], F32, kind="ExternalInput").ap()
    rope_d = dt("c_rope", [128, 2, S], F32, kind="ExternalInput").ap()
    cb_d = dt("c_cb", [128, 512], F32, kind="ExternalInput").ap()
    indk_d = dt("c_indk", [NBLK, S], F32, kind="ExternalInput").ap()
    summ_d = dt("c_summ", [64, 4 * 80], F32, kind="ExternalInput").ap()
    out_d = dt("out", [S, D], F32, kind="ExternalOutput").ap()

    with ExitStack() as es:
        P = Prog(nc, es)
        sb = lambda name, shape, dtype: es.enter_context(nc.sbuf_tensor(name, shape, dtype))
        xres = sb("xres", [128, NT, D], F32)
        xT = sb("xT", [128, NC_, S], BF16)
        lng = sb("lng", [128, D], F32)
        lnb = sb("lnb", [128, D], F32)
        ident = sb("ident", [128, 128], BF16)
        stats = sb("stats", [128, 4, 12], F32)
        mv = sb("mv", [128, 4, 2], F32)
        rstd = sb("rstd", [128, 4, 1], F32)
        ctx = {}

        def load_ln_params(idx):
            P.op("sp", lambda e: e.dma_start(out=lng[:], in_=lnp_d[2 * idx:2 * idx + 1, :].partition_broadcast(128)),
                 writes=["lng"], dma="lng")
            P.op("sp", lambda e: e.dma_start(out=lnb[:], in_=lnp_d[2 * idx + 1:2 * idx + 2, :].partition_broadcast(128)),
                 writes=["lnb"], dma="lnb")

        def xk(T):
            return [("xres", T, 0), ("xres", T, 1)]

        def ln_s1(T):
            j = T % 4
            xr = xres[:, T, :]
            P.op("dve", lambda e: e.bn_stats(stats[:, j, 0:6], xr[:, 0:512]),
                 reads=[("xres", T, 0)], writes=[("stats", j, 0)])
            P.op("dve", lambda e: e.bn_stats(stats[:, j, 6:12], xr[:, 512:1024]),
                 reads=[("xres", T, 1)], writes=[("stats", j, 1)])
            P.op("dve", lambda e: e.bn_aggr(mv[:, j, :], stats[:, j, :]),
                 reads=[("stats", j, 0), ("stats", j, 1)], writes=[("mv", j)])
            P.op("act", lambda e: e.activation(out=rstd[:, j, :], in_=mv[:, j, 1:2], func=AF.Sqrt, bias=EPS, scale=1.0),
                 reads=[("mv", j)], writes=[("rstd", j)])
            P.op("dve", lambda e: e.reciprocal(out=rstd[:, j, :], in_=rstd[:, j, :]),
                 reads=[("rstd", j)], writes=[("rstd", j)])
            P.op("dve", lambda e: e.tensor_scalar(out=xr, in0=xr, scalar1=mv[:, j, 0:1], scalar2=rstd[:, j, :],
                                                  op0=ALU.subtract, op1=ALU.mult),
                 reads=xk(T) + [("mv", j), ("rstd", j)], writes=xk(T))

        def ln_s2(T, final):
            j = T % 4
            jb = T % ctx.get("ring", 4)
            xr = xres[:, T, :]
            P.op("pool", lambda e: e.tensor_tensor(out=xr, in0=xr, in1=lng[:], op=ALU.mult),
                 reads=xk(T) + ["lng"], writes=xk(T))
            P.op("pool", lambda e: e.tensor_tensor(out=xr, in0=xr, in1=lnb[:], op=ALU.add),
                 reads=xk(T) + ["lnb"], writes=xk(T))
            if final:
                P.op("sp", lambda e: e.dma_start(out=out_d[T * 128:(T + 1) * 128, :], in_=xr),
                     reads=xk(T), dma=("out", T % 4))
                return
            xb16 = ctx["xb16"]
            P.op("act", lambda e: e.activation(out=xb16[:, jb, :], in_=xr, func=AF.Copy),
                 reads=xk(T), writes=[("xb16", jb)])

        def pipeline(n, stages, oldest_first=False):
            maxs = max(sk for sk, _ in stages)
            for step in range(n + maxs):
                for sk, fn in (sorted(stages, key=lambda st: -st[0]) if oldest_first else stages):
                    T = step - sk
                    if 0 <= T < n:
                        fn(T)

        def ln_b(T, tp):
            jb = T % ctx.get("ring", 4)
            xb16 = ctx["xb16"]
            for c in range(NC_):
                P.op("pe", lambda e, c=c: e.transpose(tp[:, c, :], xb16[:, jb, c * 128:(c + 1) * 128], ident[:]),
                     reads=[("xb16", jb), "ident"], writes=["tp"])
            P.op("act", lambda e: e.activation(out=xT[:, :, T * 128:(T + 1) * 128], in_=tp[:, :, :], func=AF.Copy),
                 reads=["tp"], writes=[("xT", T)])

        def mlp_phase(l, ln_idx, final):
            with ExitStack() as ps:
                sbp = lambda name, shape, dtype: ps.enter_context(nc.sbuf_tensor(name + "_m%d" % l, shape, dtype))
                pp = lambda name, shape, dtype: ps.enter_context(nc.psum_tensor(name + "_m%d" % l, shape, dtype))
                wq1 = sbp("wq1", [128, 2, NC_, 1024], BF16)
                wq2 = sbp("wq2", [128, 2, 8, D], BF16)
                hT = sbp("hT", [128, 2, 8, 512], BF16)
                rl = sbp("rl", [128, 3, 512], F32)
                ps_h = pp("ps_h", [128, 4, 512], F32)
                ps_o = pp("ps_o", [128, 2, 512], F32)
                tp = pp("tp", [128, NC_, 128], BF16)
                ctx["xb16"] = sbp("xb16", [128, 8, D], BF16)
                ctx["ring"] = 8
                w1v = w1_d[l].rearrange("(dc p) f -> p dc f", p=128)

                def load_q(q):
                    slot = q % 2
                    for g in range(4):
                        P.op("pool", lambda e, g=g: e.dma_start(
                            out=wq1[:, slot, :, g * 256:(g + 1) * 256],
                            in_=w1v[:, :, q * 1024 + g * 256:q * 1024 + (g + 1) * 256]),
                            writes=[("wq1", slot, g)], dma=("wq1", slot, g))
                    for g in range(4):
                        P.op("pool", lambda e, g=g: e.dma_start(
                            out=wq2[:, slot, 2 * g:2 * g + 2, :],
                            in_=w2_d[l, q * 1024 + g * 256:q * 1024 + (g + 1) * 256, :].rearrange(
                                "(fc p) d -> p fc d", p=128)),
                            writes=[("wq2", slot, g)], dma=("wq2", slot, g))

                load_ln_params(ln_idx)
                load_q(0)
                pending_b = []
                pending_s2 = []
                for q in range(4):
                    slot = q % 2
                    if q + 1 < 4:
                        load_q(q + 1)
                    for tg in range(4):
                        hb = (q * 4 + tg) % 2
                        xT_keys = [("xT", tg * 4 + i) for i in range(4)]
                        for fc in range(8):
                            bank = fc % 4
                            rb = fc % 3
                            pbank = (fc - 1) % 3
                            for dc in range(NC_):
                                P.op("pe", lambda e, fc=fc, dc=dc, bank=bank, slot=slot, tg=tg: e.matmul(
                                    ps_h[:, bank, :], wq1[:, slot, dc, fc * 128:(fc + 1) * 128],
                                    xT[:, dc, tg * 512:(tg + 1) * 512], start=(dc == 0), stop=(dc == NC_ - 1)),
                                    reads=[("wq1", slot, fc // 2)] + xT_keys, writes=[("ps_h", bank)])
                            P.op("act", lambda e, bank=bank, rb=rb: e.activation(out=rl[:, rb, :], in_=ps_h[:, bank, :],
                                                                                  func=AF.Relu),
                                 reads=[("ps_h", bank)], writes=[("rl", rb)])
                            if fc > 0:
                                P.op("act", lambda e, fc=fc, pbank=pbank, hb=hb: e.activation(
                                    out=hT[:, hb, fc - 1, :], in_=rl[:, pbank, :], func=AF.Square),
                                    reads=[("rl", pbank)], writes=[("hT", hb, fc - 1)])
                        P.op("act", lambda e, hb=hb: e.activation(out=hT[:, hb, 7, :], in_=rl[:, 1, :], func=AF.Square),
                             reads=[("rl", 1)], writes=[("hT", hb, 7)])
                        for tt in range(4):
                            T = tg * 4 + tt
                            for half in range(2):
                                ob = (tt * 2 + half) % 2
                                for fc in range(8):
                                    P.op("pe", lambda e, fc=fc, tt=tt, half=half, ob=ob, hb=hb, slot=slot: e.matmul(
                                        ps_o[:, ob, :], hT[:, hb, fc, tt * 128:(tt + 1) * 128],
                                        wq2[:, slot, fc, half * 512:(half + 1) * 512], start=(fc == 0), stop=(fc == 7)),
                                        reads=[("hT", hb, fc), ("wq2", slot, fc // 2)], writes=[("ps_o", ob)])
                                xs = xres[:, T, half * 512:(half + 1) * 512]
                                if q == 0:
                                    P.op("dve", lambda e, xs=xs, ob=ob: e.scalar_tensor_tensor(
                                        out=xs, in0=xs, scalar=ALPHA, in1=ps_o[:, ob, :], op0=ALU.mult, op1=ALU.add),
                                        reads=[("xres", T, half), ("ps_o", ob)], writes=[("xres", T, half)])
                                else:
                                    P.op("dve", lambda e, xs=xs, ob=ob: e.tensor_tensor(
                                        out=xs, in0=xs, in1=ps_o[:, ob, :], op=ALU.add),
                                        reads=[("xres", T, half), ("ps_o", ob)], writes=[("xres", T, half)])
                            if q == 3:
                                ln_s1(T)
                                for T2 in pending_s2:
                                    ln_s2(T2, final)
                                    if not final:
                                        pending_b.append(T2)
                                pending_s2 = [T]
                                if pending_b and pending_b[0] < tg * 4:
                                    ln_b(pending_b.pop(0), tp)
                for T in pending_b:
                    ln_b(T, tp)
                for T2 in pending_s2:
                    ln_s2(T2, final)
                    if not final:
                        ln_b(T2, tp)
                if stop_after is not None and stop_after == ln_idx:
                    return True
                P.flush()
            return False

        def attn_phase():
            with ExitStack() as po:
                OT = po.enter_context(nc.sbuf_tensor("OT", [128, NC_, S], BF16))
                with ExitStack() as ps:
                    sbp = lambda name, shape, dtype: ps.enter_context(nc.sbuf_tensor(name, shape, dtype))
                    pp = lambda name, shape, dtype: ps.enter_context(nc.psum_tensor(name, shape, dtype))
                    wpair = sbp("wpair", [128, 3, NC_, 128], BF16)
                    permm = sbp("permm", [128, 128], BF16)
                    qb = sbp("qb", [128, 2, 512], BF16)
                    Qz = sbp("Qz", [128, 2, S], BF16)
                    Kz = sbp("Kz", [128, 2, S], BF16)
                    Vp = sbp("Vp", [128, NT, 2, 128], BF16)
                    rope = sbp("rope", [128, 2, 2, 512], F32)
                    t1 = sbp("t1", [128, 2, 512], F32)
                    t2 = sbp("t2", [128, 2, 512], F32)
                    PT = sbp("PT", [128, 6, 512], BF16)
                    kms = sbp("kms", [128, NBLK], F32)
                    KMd = sbp("KMd", [128, 128], BF16)
                    G = sbp("G", [64, 512], BF16)
                    CB = sbp("CB", [128, 2, 256], BF16)
                    SUMM = sbp("SUMM", [64, 4, 80], BF16)
                    den = sbp("den", [128, 2, 512], F32)
                    NWB = 5
                    ps_proj = pp("ps_proj", [128, NWB, 512], F32)
                    ps_s = ps_proj
                    ps_o = pp("ps_o2", [128, 2, 512], F32)
                    ps_sel = pp("ps_sel", [128, 512], F32)

                    P.op("pool", lambda e: e.dma_start(out=CB[:].rearrange("p a b -> p (a b)"), in_=cb_d[:, :]),
                         writes=["CB"], dma="CB")
                    P.op("pool", lambda e: e.memset(Qz[:].rearrange("p a b -> p (a b)"), 0.0), writes=["Qz0"])
                    P.op("pool", lambda e: e.memset(Kz[:].rearrange("p a b -> p (a b)"), 0.0), writes=["Kz0"])
                    P.op("pool", lambda e: e.dma_start(out=Kz[64:72, 0, :], in_=indk_d[:, :]),
                         reads=["Kz0"], writes=["Kzi0"], dma="Kzi0")
                    P.op("pool", lambda e: e.dma_start(out=Kz[0:8, 1, :], in_=indk_d[:, :]),
                         reads=["Kz0"], writes=["Kzi1"], dma="Kzi1")
                    P.op("pool", lambda e: e.dma_start(out=SUMM[:].rearrange("p a b -> p (a b)"), in_=summ_d[:, :]),
                         writes=["SUMM"], dma="SUMM")
                    P.op("pool", lambda e: e.dma_start(out=permm[:], in_=perm_d[:, :]), writes=["permm"], dma="permm")
                    P.op("pool", lambda e: e.memset(Vp[:, :, 0, 64:128], 1.0), writes=["Vp1a"])
                    P.op("pool", lambda e: e.memset(Vp[:, :, 1, 0:64], 1.0), writes=["Vp1b"])
                    P.op("pool", lambda e: e.memset(KMd[:], 0.0), writes=["KMd0"])

                    cnt = {"pj": 0, "t": 0, "s": 0, "o": 0, "rope": 0}

                    def next_bank():
                        b = cnt["pj"] % NWB
                        cnt["pj"] += 1
                        return b

                    def qk_item(c, tg):
                        rs = cnt["rope"] % 2
                        cnt["rope"] += 1
                        xT_keys = [("xT", tg * 4 + i) for i in range(4)]
                        P.op("sp", lambda e: e.dma_start(
                            out=rope[:, rs, :, :], in_=rope_d[:, :, tg * 512:(tg + 1) * 512]),
                            writes=[("rope", rs)], dma=("rope", rs))
                        banks = []
                        for which in (0, 1):
                            ba = next_bank()
                            banks.append(ba)
                            for dc in range(NC_):
                                P.op("pe", lambda e, which=which, ba=ba, dc=dc: e.matmul(
                                    ps_proj[:, ba, :], wpair[:, which, dc, :], xT[:, dc, tg * 512:(tg + 1) * 512],
                                    start=(dc == 0), stop=(dc == NC_ - 1)),
                                    reads=["wpair"] + xT_keys, writes=[("pp", ba)])
                            P.op("act", lambda e, which=which, ba=ba: e.activation(
                                out=qb[:, which, :], in_=ps_proj[:, ba, :], func=AF.Copy),
                                reads=[("pp", ba)], writes=[("qb", which)])
                        for which in (0, 1):
                            ba = banks[which]
                            bb = next_bank()
                            P.op("pe", lambda e, which=which, bb=bb: e.matmul(
                                ps_proj[:, bb, :], permm[:], qb[:, which, :], start=True, stop=True),
                                reads=["permm", ("qb", which)], writes=[("pp", bb)])
                            ts = cnt["t"] % 2
                            cnt["t"] += 1
                            P.op("dve", lambda e, ts=ts, ba=ba: e.tensor_tensor(
                                out=t1[:, ts, :], in0=ps_proj[:, ba, :], in1=rope[:, rs, 0, :], op=ALU.mult),
                                reads=[("pp", ba), ("rope", rs)], writes=[("t1", ts)])
                            P.op("dve", lambda e, ts=ts, bb=bb: e.tensor_tensor(
                                out=t2[:, ts, :], in0=ps_proj[:, bb, :], in1=rope[:, rs, 1, :], op=ALU.mult),
                                reads=[("pp", bb), ("rope", rs)], writes=[("t2", ts)])
                            dstz, dkey, zkey = (Qz, "Qp", "Qz0") if which == 0 else (Kz, "Kp", "Kz0")
                            for hh in (0, 1):
                                rr = slice(hh * 64, hh * 64 + 64)
                                P.op("dve", lambda e, ts=ts, hh=hh, rr=rr, dstz=dstz: e.tensor_tensor(
                                    out=dstz[rr, hh, tg * 512:(tg + 1) * 512], in0=t1[rr, ts, :], in1=t2[rr, ts, :],
                                    op=ALU.add),
                                    reads=[("t1", ts), ("t2", ts), zkey], writes=[(dkey, hh, tg)])
                                if which == 1:
                                    P.op("dve", lambda e, hh=hh, rr=rr: e.tensor_reduce(
                                        out=kms[rr, 2 * tg:2 * tg + 2],
                                        in_=Kz[rr, hh, tg * 512:(tg + 1) * 512].rearrange("p (b k) -> p b k", b=2),
                                        axis=AX.X, op=ALU.add),
                                        reads=[("Kp", hh, tg)], writes=[("kms", hh, tg)])

                    def v_item(c, g4):
                        bv = next_bank()
                        for i in range(4):
                            T = g4 * 4 + i
                            for dc in range(NC_):
                                P.op("pe", lambda e, i=i, T=T, dc=dc: e.matmul(
                                    ps_proj[:, bv, i * 128:(i + 1) * 128], xT[:, dc, T * 128:(T + 1) * 128],
                                    wpair[:, 2, dc, :], start=(dc == 0), stop=(dc == NC_ - 1)),
                                    reads=["wpair", ("xT", T)], writes=[("pp", bv)])
                        pv = ps_proj[:, bv, :].rearrange("p (a b) -> p a b", a=4)
                        P.op("dve", lambda e: e.tensor_copy(
                            out=Vp[:, g4 * 4:(g4 + 1) * 4, 0, 0:64], in_=pv[:, :, 0:64]),
                            reads=[("pp", bv)], writes=[("Vp", g4, 0)])
                        P.op("dve", lambda e: e.tensor_copy(
                            out=Vp[:, g4 * 4:(g4 + 1) * 4, 1, 64:128], in_=pv[:, :, 64:128]),
                            reads=[("pp", bv)], writes=[("Vp", g4, 1)])

                    def kmd_item(c):
                        for hh in (0, 1):
                            r0 = hh * 64
                            P.op("dve", lambda e, r0=r0: e.tensor_tensor(
                                out=KMd[r0:r0 + 64, r0:r0 + 64].rearrange("p (n m) -> p n m", n=NBLK),
                                in0=kms[r0:r0 + 64, :].unsqueeze(1).broadcast_to([64, NBLK, NBLK]),
                                in1=kms[r0:r0 + 64, :].unsqueeze(2).broadcast_to([64, NBLK, NBLK]),
                                op=ALU.subtract),
                                reads=[("kms", hh, i) for i in range(4)] + ["KMd0"], writes=[("KMd", hh)])

                    def sel_a(c, own):
                        jq = own // 2
                        for hh in (0, 1):
                            P.op("pe", lambda e, hh=hh: e.matmul(
                                ps_sel[0:64, hh * 256:(hh + 1) * 256], KMd[:, hh * 64:(hh + 1) * 64],
                                Qz[:, hh, own * 256:(own + 1) * 256], start=True, stop=True),
                                reads=[("KMd", hh), "KMd0", ("Qp", hh, jq), "Qz0", ("Qzb", hh, own)], writes=["selbank"])
                        P.op("dve", lambda e: e.tensor_single_scalar(out=G[:], in_=ps_sel[0:64, :], scalar=0.0,
                                                                     op=ALU.is_gt),
                             reads=["selbank"], writes=["G"])

                    def sel_b(c, own):
                        P.op("pe", lambda e: e.matmul(
                            ps_sel[0:72, 0:256], SUMM[:, own - 4, 0:72], G[:, 0:256], start=True, stop=True),
                            reads=["G", "SUMM", "selbank"], writes=["selbank"])
                        P.op("pe", lambda e: e.matmul(
                            ps_sel[0:8, 256:512], SUMM[:, own - 4, 72:80], G[:, 256:512], start=True, stop=True),
                            reads=["G", "SUMM", "selbank"], writes=["selbank"])
                        P.op("dve", lambda e: e.tensor_scalar(
                            out=Qz[64:72, 0, own * 256:(own + 1) * 256], in0=ps_sel[64:72, 0:256], scalar1=2.5,
                            scalar2=NEG, op0=ALU.is_ge, op1=ALU.mult),
                            reads=["selbank", "Qz0"], writes=[("Qzb", 0, own)])
                        P.op("dve", lambda e: e.tensor_scalar(
                            out=Qz[0:8, 1, own * 256:(own + 1) * 256], in0=ps_sel[0:8, 256:512], scalar1=2.5,
                            scalar2=NEG, op0=ALU.is_ge, op1=ALU.mult),
                            reads=["selbank", "Qz0"], writes=[("Qzb", 1, own)])

                    def emit_s(c, it):
                        hh, j, kt, kind = it
                        wbk = next_bank()
                        pslot = cnt["s"] % 6
                        cnt["s"] += 1
                        kap = Kz[:, hh, kt * 128:(kt + 1) * 128]
                        kk = [("Kp", hh, kt // 4), "Kz0", "Kzi0", "Kzi1", ("Qp", hh, j), "Qz0"]
                        kk += [("Qzb", hh, o) for o in (2 * j, 2 * j + 1) if o >= 4]
                        if kind == "diagB":
                            cols = slice(256, 512)
                            P.op("pe", lambda e: e.matmul(
                                ps_s[:, wbk, 256:512], kap, Qz[:, hh, j * 512 + 256:(j + 1) * 512],
                                start=True, stop=False), reads=kk, writes=[("pp", wbk)])
                            P.op("pe", lambda e: e.matmul(
                                ps_s[:, wbk, 256:512], ident[:], CB[:, kt % 2, :], start=False, stop=True),
                                reads=["ident", "CB"], writes=[("pp", wbk)])
                        else:
                            cols = slice(0, 512)
                            P.op("pe", lambda e: e.matmul(
                                ps_s[:, wbk, :], kap, Qz[:, hh, j * 512:(j + 1) * 512], start=True,
                                stop=(kind == "full")), reads=kk, writes=[("pp", wbk)])
                            if kind == "mixed":
                                P.op("pe", lambda e: e.matmul(
                                    ps_s[:, wbk, 0:256], ident[:], CB[:, kt % 2, :], start=False, stop=True),
                                    reads=["ident", "CB"], writes=[("pp", wbk)])
                        P.op("act", lambda e: e.activation(out=PT[:, pslot, cols], in_=ps_s[:, wbk, cols],
                                                           func=AF.Exp, scale=DH ** -0.5),
                             reads=[("pp", wbk)], writes=[("PT", pslot)])
                        return pslot

                    def emit_pv(c, it, pslot, ob):
                        hh, j, kt, kind = it
                        cols = slice(256, 512) if kind == "diagB" else slice(0, 512)
                        first = (kt == 0)
                        last = (kt == 4 * j + 3)
                        P.op("pe", lambda e: e.matmul(
                            ps_o[:, ob, cols], Vp[:, kt, hh, :], PT[:, pslot, cols], start=first, stop=last),
                            reads=[("Vp", kt // 4, hh), "Vp1a", "Vp1b", ("PT", pslot)], writes=[("ps_o", ob)])
                        if last:
                            orow = slice(0, 64) if hh == 0 else slice(64, 128)
                            drow = slice(64, 128) if hh == 0 else slice(0, 64)
                            P.op("dve", lambda e: e.reciprocal(out=den[drow, ob, :], in_=ps_o[drow, ob, :]),
                                 reads=[("ps_o", ob)], writes=[("den", ob)])
                            P.op("dve", lambda e: e.tensor_tensor(
                                out=OT[orow, c, j * 512:(j + 1) * 512], in0=ps_o[orow, ob, :],
                                in1=den[drow, ob, :], op=ALU.mult),
                                reads=[("ps_o", ob), ("den", ob)], writes=[("OT", c, hh, j)])

                    DEPTH = 4
                    for c in range(NC_):
                        P.op("pool", lambda e, c=c: e.dma_start(
                            out=wpair[:].rearrange("p a b c -> p (a b c)"), in_=wattn_d[c]),
                            writes=["wpair"], dma="wpair")
                        for tg in range(4):
                            qk_item(c, tg)
                        for g4 in range(4):
                            v_item(c, g4)
                        items = []
                        for hh in (0, 1):
                            for j in range(4):
                                items += [(hh, j, kt, "full") for kt in range(4 * j)]
                                items += [(hh, j, kt, "mixed") for kt in (4 * j, 4 * j + 1)]
                                items += [(hh, j, kt, "diagB") for kt in (4 * j + 2, 4 * j + 3)]
                        hooks = {
                            0: [lambda c=c: kmd_item(c), lambda c=c: sel_a(c, 4)],
                            2: [lambda c=c: sel_b(c, 4), lambda c=c: sel_a(c, 5)],
                            4: [lambda c=c: sel_b(c, 5), lambda c=c: sel_a(c, 6)],
                            6: [lambda c=c: sel_b(c, 6), lambda c=c: sel_a(c, 7)],
                            8: [lambda c=c: sel_b(c, 7)],
                        }
                        gid = {}
                        for it in items:
                            if (it[0], it[1]) not in gid:
                                gid[(it[0], it[1])] = cnt["o"]
                                cnt["o"] += 1
                        pslots = {}
                        for i in range(len(items) + DEPTH):
                            for f in hooks.get(i, ()):
                                f()
                            if i < len(items):
                                pslots[i] = emit_s(c, items[i])
                            if i >= DEPTH:
                                it = items[i - DEPTH]
                                emit_pv(c, it, pslots[i - DEPTH], gid[(it[0], it[1])] % 2)
                    P.flush()
                with ExitStack() as ps:
                    sbp = lambda name, shape, dtype: ps.enter_context(nc.sbuf_tensor(name, shape, dtype))
                    pp = lambda name, shape, dtype: ps.enter_context(nc.psum_tensor(name, shape, dtype))
                    wo = sbp("wo", [128, NC_, D], BF16)
                    ctx["xb16"] = sbp("xb16_a", [128, 4, D], BF16)
                    ctx["ring"] = 4
                    ps_y = pp("ps_y2", [128, 2, 512], F32)
                    tp = pp("tp_a", [128, NC_, 128], BF16)
                    for g in range(4):
                        P.op("pool", lambda e, g=g: e.dma_start(
                            out=wo[:, 2 * g:2 * g + 2, :],
                            in_=wo_d[g * 256:(g + 1) * 256, :].rearrange("(c p) e -> p c e", p=128)),
                            writes=[("wo", g)], dma=("wo", g))
                    load_ln_params(2)

                    def wo_s0(T):
                        for half in range(2):
                            ob = (T * 2 + half) % 2
                            for c in range(NC_):
                                P.op("pe", lambda e, half=half, ob=ob, c=c: e.matmul(
                                    ps_y[:, ob, :], OT[:, c, T * 128:(T + 1) * 128],
                                    wo[:, c, half * 512:(half + 1) * 512], start=(c == 0), stop=(c == NC_ - 1)),
                                    reads=[("wo", c // 2)], writes=[("ps_y", ob)])
                            xs = xres[:, T, half * 512:(half + 1) * 512]
                            P.op("dve", lambda e, xs=xs, ob=ob: e.scalar_tensor_tensor(
                                out=xs, in0=xs, scalar=ALPHA, in1=ps_y[:, ob, :], op0=ALU.mult, op1=ALU.add),
                                reads=[("xres", T, half), ("ps_y", ob)], writes=[("xres", T, half)])

                    pipeline(NT, [(0, wo_s0), (2, ln_s1), (4, lambda T: ln_s2(T, False)), (6, lambda T: ln_b(T, tp))])
                    if stop_after == 2:
                        return True
                    P.flush()
            return False

        def dump_and_finish():
            for T in range(NT):
                P.op("sp", lambda e, T=T: e.dma_start(out=out_d[T * 128:(T + 1) * 128, :], in_=xres[:, T, :]),
                     reads=xk(T), dma=("out", T % 4))
            P.flush()
            P.flush(final=True)
            return nc

        with ExitStack() as ps:
            sbp = lambda name, shape, dtype: ps.enter_context(nc.sbuf_tensor(name, shape, dtype))
            pp = lambda name, shape, dtype: ps.enter_context(nc.psum_tensor(name, shape, dtype))
            xbf = sbp("xbf", [128, 3, D], BF16)
            pooled = sbp("pooled", [128, 2, NC_, 128], BF16)
            band = sbp("band", [128, 1536], BF16)
            pw_stage = sbp("pw_stage", [128, 4, 2, 256], F32)
            sc_tab = sbp("sc_tab", [128, D], F32)
            poolw = sbp("poolw", [128, 4, 2, 256], BF16)
            ps_pool = pp("ps_pool", [128, NC_, 128], F32)
            ps_y = pp("ps_y", [128, 2, D], F32)
            tp = pp("tp", [128, NC_, 128], BF16)
            ctx["xb16"] = sbp("xb16_p0", [128, 4, D], BF16)
            ctx["ring"] = 4

            load_ln_params(0)
            P.op("sp", lambda e: e.dma_start(out=sc_tab[:], in_=pscale_d[0:1, :].partition_broadcast(128)),
                 writes=["sc_tab"], dma="sc_tab")
            P.op("sp", lambda e: e.dma_start(out=pw_stage[:], in_=poolw_d.rearrange("g (k p) e -> p g k e", p=128)),
                 writes=["pw_stage"], dma="pw_stage")
            P.op("pool", lambda e: e.dma_start(out=band[:], in_=band_d[:, :]), writes=["band"], dma="band")
            P.op("pool", lambda e: e.dma_start(out=ident[:], in_=ident_d[:, :]), writes=["ident"], dma="ident")
            for k in range(2):
                P.op("dve", lambda e, k=k: e.tensor_tensor(
                    out=poolw[:, :, k, :], in0=pw_stage[:, :, k, :],
                    in1=sc_tab[:].rearrange("p (g e) -> p g e", g=4), op=ALU.mult),
                    reads=["pw_stage", "sc_tab"], writes=[("poolw", k)])
            for T in range(NT):
                P.op("sp", lambda e, T=T: e.dma_start(out=xres[:, T, :], in_=x_d[T * 128:(T + 1) * 128, :]),
                     writes=xk(T), dma=("xld", T))

            def load_xbf(T):
                P.op("pool", lambda e, T=T: e.dma_start(out=xbf[:, T % 3, :], in_=x_d[T * 128:(T + 1) * 128, :]),
                     writes=[("xbf", T % 3)], dma=("xbf", T % 3))

            load_xbf(0)
            load_xbf(1)

            def p0_s0(T):
                for c in range(NC_):
                    g = c // 2
                    boff = (512 if T == 0 else 0) + g * 128
                    P.op("pe", lambda e, c=c, boff=boff: e.matmul(
                        ps_pool[:, c, :], xbf[:, T % 3, c * 128:(c + 1) * 128], band[:, boff:boff + 128],
                        start=True, stop=(T == 0)),
                        reads=[("xbf", T % 3), "band"], writes=["ps_pool"])
                    if T > 0:
                        P.op("pe", lambda e, c=c, g=g: e.matmul(
                            ps_pool[:, c, :], xbf[64:128, (T - 1) % 3, c * 128:(c + 1) * 128],
                            band[64:128, 1024 + g * 128:1024 + (g + 1) * 128], start=False, stop=True),
                            reads=[("xbf", (T - 1) % 3), "band"], writes=["ps_pool"])
                if T + 2 < NT:
                    load_xbf(T + 2)
                P.op("act", lambda e: e.activation(out=pooled[:, T % 2, :, :], in_=ps_pool[:, :, :], func=AF.Copy),
                     reads=["ps_pool"], writes=[("pooled", T % 2)])
                for g in range(4):
                    for k in range(2):
                        P.op("pe", lambda e, g=g, k=k: e.matmul(
                            ps_y[:, T % 2, g * 256:(g + 1) * 256], pooled[:, T % 2, 2 * g + k, :], poolw[:, g, k, :],
                            start=(k == 0), stop=(k == 1)),
                            reads=[("pooled", T % 2), ("poolw", k)], writes=[("ps_y", T % 2)])
                P.op("dve", lambda e: e.scalar_tensor_tensor(
                    out=xres[:, T, :], in0=xres[:, T, :], scalar=ALPHA, in1=ps_y[:, T % 2, :],
                    op0=ALU.mult, op1=ALU.add),
                    reads=xk(T) + [("ps_y", T % 2)], writes=xk(T))

            if dbg == 'noln':
                pipeline(NT, [(0, p0_s0)])
            else:
                pipeline(NT, [(0, p0_s0), (2, ln_s1), (4, lambda T: ln_s2(T, False)), (6, lambda T: ln_b(T, tp))],
                         oldest_first=True)
            if stop_after == 0:
                return dump_and_finish()
            P.flush()

        mlp_phase(0, 1, False)
        if stop_after == 1:
            return dump_and_finish()
        attn_phase()
        if stop_after == 2:
            return dump_and_finish()
        mlp_phase(1, 3, True)
        P.flush(final=True)
    return nc


def _consts():
    band = np.zeros((128, 1536), np.float32)
    tl = np.arange(128)
    for g, w in enumerate((2, 4, 8, 16)):
        dd = tl[None, :] - tl[:, None]
        inwin = (dd >= 0) & (dd < w)
        Dg = inwin.astype(np.float32) / w - np.eye(128, dtype=np.float32)
        cnt = np.minimum(tl + 1, w).astype(np.float32)
        D0 = inwin.astype(np.float32) / cnt[None, :] - np.eye(128, dtype=np.float32)
        dd2 = tl[None, :] + 128 - tl[:, None]
        U = ((dd2 >= 0) & (dd2 < w)).astype(np.float32) / w
        band[:, g * 128:(g + 1) * 128] = Dg
        band[:, 512 + g * 128:512 + (g + 1) * 128] = D0
        band[:, 1024 + g * 128:1024 + (g + 1) * 128] = U
    half = DH // 2
    inv = (np.float32(10000.0) ** (-np.arange(half, dtype=np.float32) * np.float32(2.0) / np.float32(DH))).astype(np.float32)
    ang = (np.arange(S, dtype=np.float32)[None, :] * inv[:, None]).astype(np.float32)
    cos = np.cos(ang).astype(np.float32)
    sin = np.sin(ang).astype(np.float32)
    rope = np.zeros((128, 2, S), np.float32)
    for p in range(128):
        rope[p, 0] = cos[p % 32]
        rope[p, 1] = -sin[p % 32] if (p % 64) < 32 else sin[p % 32]
    kk = np.arange(128)[:, None]
    qq = np.arange(256)[None, :]
    cb = np.zeros((128, 2, 256), np.float32)
    cb[:, 0] = np.where(kk > qq, NEG, 0.0)
    cb[:, 1] = np.where(kk + 128 > qq, NEG, 0.0)
    indk = np.zeros((NBLK, S), np.float32)
    for n in range(NBLK):
        indk[n, n * BLK:(n + 1) * BLK] = 1.0
    summ = np.zeros((64, 4, 80), np.float32)
    for own in range(4, NBLK):
        for n in range(own):
            for m in range(own):
                summ[n * NBLK + m, own - 4, 64 + n] = 1.0
                summ[n * NBLK + m, own - 4, 72 + n] = 1.0
    pm = np.zeros((128, 128), np.float32)
    for p in range(128):
        pm[p + 32 if (p % 64) < 32 else p - 32, p] = 1.0
    return {"c_band": band, "c_ident": np.eye(128, dtype=np.float32), "c_rope": rope, "c_perm": pm,
            "c_cb": cb.reshape(128, 512), "c_indk": indk, "c_summ": summ.reshape(64, 320)}


def make_in_maps(x, pool_w, pool_scale, mlp_w1, mlp_w2, ln_mix_g, ln_mix_b, ln_ffn_g, ln_ffn_b, w_kv, w_q, w_o):
    c = _consts()
    lnp = np.ascontiguousarray(np.stack([ln_mix_g[0], ln_mix_b[0], ln_ffn_g[0], ln_ffn_b[0],
                                         ln_mix_g[1], ln_mix_b[1], ln_ffn_g[1], ln_ffn_b[1]], 0), dtype=np.float32)
    shared = {
        "pool_w": np.ascontiguousarray(pool_w[0], dtype=np.float32),
        "pool_scale": np.ascontiguousarray(pool_scale[0:1], dtype=np.float32),
        "lnp": lnp,
        "mlp_w1": np.ascontiguousarray(mlp_w1, dtype=np.float32),
        "mlp_w2": np.ascontiguousarray(mlp_w2, dtype=np.float32),
        "w_o": np.ascontiguousarray(w_o[0], dtype=np.float32),
    }
    wq = np.asarray(w_q[0], dtype=np.float32)
    wk = np.asarray(w_kv[:, :D], dtype=np.float32)
    wv = np.asarray(w_kv[:, D:], dtype=np.float32)
    mats = np.stack([wq, wk, wv], 0)
    wattn = mats.reshape(3, NC_, 128, NC_, 128).transpose(3, 2, 0, 1, 4)
    shared["wattn"] = np.ascontiguousarray(wattn).reshape(NC_, 128, 3 * NC_ * 128)
    shared.update(c)
    maps = []
    for i in range(x.shape[0]):
        m = dict(shared)
        m["x"] = np.ascontiguousarray(x[i], dtype=np.float32)
        maps.append(m)
    return maps


def kernel(**inputs):
    inputs = {k: np.asarray(v) for k, v in inputs.items()}
    nc = build_program()
    in_maps = make_in_maps(**inputs)
    res = run_bass_kernel_spmd(nc, in_maps, core_ids=list(range(N_CORES)))
    return np.stack([r["out"] for r in res.results], 0).astype(np.float32)
```
